# Optimizing a Trainium2 kernel written in Bass

```python
import math
import jax, jax.numpy as jnp
from jax import lax
import numpy as np

D_MODEL = 1024
BATCH = 8
SEQ = 8192
DEPTH = 1

N_META = 16
D_MIX = D_MODEL
D_LRU = D_MIX // 2
D_CC = D_MIX - D_LRU
LRU_HEADS = 8
LRU_HEAD_DIM = D_LRU // LRU_HEADS
CC_GROUPS = 8
LRU_CONV_W = 4
CC_CONV_W = 31
LRU_C = 8.0
D_FF = 4 * D_MODEL
EPS = 1e-6

kernel_name = "hymba_style_rglru_conformer_conv_hybrid"


def rmsnorm(x, g):
    xf = x.astype(jnp.float32)
    y = xf * lax.rsqrt(jnp.mean(xf * xf, axis=-1, keepdims=True) + EPS)
    return (y * g.astype(jnp.float32)).astype(x.dtype)


def layernorm(x, g, b):
    xf = x.astype(jnp.float32)
    mu = jnp.mean(xf, axis=-1, keepdims=True)
    var = jnp.mean(jnp.square(xf - mu), axis=-1, keepdims=True)
    y = (xf - mu) * lax.rsqrt(var + EPS)
    return (y * g.astype(jnp.float32) + b.astype(jnp.float32)).astype(x.dtype)


def causal_depthwise_conv(x, w, b):
    k = w.shape[0]
    y = lax.conv_general_dilated(
        x, w[:, None, :].astype(x.dtype), window_strides=(1,), padding=[(k - 1, 0)],
        dimension_numbers=("NWC", "WIO", "NWC"), feature_group_count=x.shape[-1])
    return y + b.astype(x.dtype)


def block_diag_linear(x, w, b):
    B, T, _ = x.shape
    h, dh, _ = w.shape
    y = jnp.einsum("bthi,hij->bthj", x.reshape(B, T, h, dh), w.astype(x.dtype))
    return y.reshape(B, T, h * dh) + b.astype(x.dtype)


def rg_lru(x, w_a, b_a, w_x, b_x, a_param):
    xf = x.astype(jnp.float32)
    r = jax.nn.sigmoid(block_diag_linear(xf, w_a, b_a))
    i = jax.nn.sigmoid(block_diag_linear(xf, w_x, b_x))
    log_a = -LRU_C * r * jax.nn.softplus(-a_param.astype(jnp.float32))
    a = jnp.exp(log_a)
    mult = jnp.sqrt(-jnp.expm1(2.0 * log_a))
    bterm = mult * (i * xf)

    def combine(left, right):
        a1, b1 = left
        a2, b2 = right
        return a1 * a2, a2 * b1 + b2

    _, h = lax.associative_scan(combine, (a, bterm), axis=1)
    return h.astype(x.dtype)


def setup_inputs(seed: int = 0) -> dict:
    key = jax.random.key(seed)
    ks = jax.random.split(key, 24)
    f32 = jnp.float32

    def nrm(k, shape, scale):
        return jax.random.normal(k, shape, f32) * scale

    x = jax.random.normal(ks[0], (BATCH, SEQ, D_MODEL), f32)
    meta_tokens = nrm(ks[1], (N_META, D_MODEL), 1.0)
    norm_mix_g = 1.0 + nrm(ks[2], (DEPTH, D_MODEL), 0.02)
    w_in = nrm(ks[3], (DEPTH, D_MODEL, 2 * D_LRU + 2 * D_CC), D_MODEL ** -0.5)
    lru_conv_w = nrm(ks[4], (DEPTH, LRU_CONV_W, D_LRU), LRU_CONV_W ** -0.5)
    lru_conv_b = nrm(ks[5], (DEPTH, D_LRU), 0.01)
    lru_gate_a_w = nrm(ks[6], (DEPTH, LRU_HEADS, LRU_HEAD_DIM, LRU_HEAD_DIM), LRU_HEAD_DIM ** -0.5)
    lru_gate_a_b = nrm(ks[7], (DEPTH, D_LRU), 0.01)
    lru_gate_x_w = nrm(ks[8], (DEPTH, LRU_HEADS, LRU_HEAD_DIM, LRU_HEAD_DIM), LRU_HEAD_DIM ** -0.5)
    lru_gate_x_b = nrm(ks[9], (DEPTH, D_LRU), 0.01)
    a_c = jax.random.uniform(ks[10], (DEPTH, D_LRU), f32, minval=0.9, maxval=0.999)
    a_base = jnp.power(a_c, 1.0 / LRU_C)
    lru_a_param = jnp.log(a_base) - jnp.log1p(-a_base)
    cc_conv_w = nrm(ks[11], (DEPTH, CC_CONV_W, D_CC), CC_CONV_W ** -0.5)
    cc_conv_b = nrm(ks[12], (DEPTH, D_CC), 0.01)
    cc_ln_g = 1.0 + nrm(ks[13], (DEPTH, D_CC), 0.02)
    cc_ln_b = nrm(ks[14], (DEPTH, D_CC), 0.01)
    out_norm_lru_g = 1.0 + nrm(ks[15], (DEPTH, D_LRU), 0.02)
    out_norm_cc_g = 1.0 + nrm(ks[16], (DEPTH, D_CC), 0.02)
    w_out = nrm(ks[17], (DEPTH, D_MIX, D_MODEL), D_MIX ** -0.5)
    norm_mlp_g = 1.0 + nrm(ks[18], (DEPTH, D_MODEL), 0.02)
    w_mlp_up = nrm(ks[19], (DEPTH, D_MODEL, D_FF), D_MODEL ** -0.5)
    w_mlp_down = nrm(ks[20], (DEPTH, D_FF, D_MODEL), D_FF ** -0.5)
    norm_final_g = 1.0 + nrm(ks[21], (D_MODEL,), 0.02)
    return {
        "x": x, "meta_tokens": meta_tokens, "norm_mix_g": norm_mix_g, "w_in": w_in,
        "lru_conv_w": lru_conv_w, "lru_conv_b": lru_conv_b,
        "lru_gate_a_w": lru_gate_a_w, "lru_gate_a_b": lru_gate_a_b,
        "lru_gate_x_w": lru_gate_x_w, "lru_gate_x_b": lru_gate_x_b,
        "lru_a_param": lru_a_param, "cc_conv_w": cc_conv_w, "cc_conv_b": cc_conv_b,
        "cc_ln_g": cc_ln_g, "cc_ln_b": cc_ln_b,
        "out_norm_lru_g": out_norm_lru_g, "out_norm_cc_g": out_norm_cc_g, "w_out": w_out,
        "norm_mlp_g": norm_mlp_g, "w_mlp_up": w_mlp_up, "w_mlp_down": w_mlp_down,
        "norm_final_g": norm_final_g,
    }


def reference(x, meta_tokens, norm_mix_g, w_in, lru_conv_w, lru_conv_b, lru_gate_a_w,
              lru_gate_a_b, lru_gate_x_w, lru_gate_x_b, lru_a_param, cc_conv_w, cc_conv_b,
              cc_ln_g, cc_ln_b, out_norm_lru_g, out_norm_cc_g, w_out, norm_mlp_g,
              w_mlp_up, w_mlp_down, norm_final_g):
    B = x.shape[0]
    meta = jnp.broadcast_to(meta_tokens[None].astype(x.dtype), (B, N_META, D_MODEL))
    h = jnp.concatenate([meta, x], axis=1)

    for l in range(DEPTH):
        u = rmsnorm(h, norm_mix_g[l])
        proj = u @ w_in[l].astype(u.dtype)
        x_lru, g_lru, v_cc, g_cc = jnp.split(
            proj, [D_LRU, 2 * D_LRU, 2 * D_LRU + D_CC], axis=-1)

        xc = causal_depthwise_conv(x_lru, lru_conv_w[l], lru_conv_b[l])
        y_lru = rg_lru(xc, lru_gate_a_w[l], lru_gate_a_b[l], lru_gate_x_w[l],
                       lru_gate_x_b[l], lru_a_param[l]) * jax.nn.gelu(g_lru)

        c = v_cc * jax.nn.sigmoid(g_cc)
        c = causal_depthwise_conv(c, cc_conv_w[l], cc_conv_b[l])
        y_cc = jax.nn.silu(layernorm(c, cc_ln_g[l], cc_ln_b[l]))

        y = jnp.concatenate([rmsnorm(y_lru, out_norm_lru_g[l]),
                             rmsnorm(y_cc, out_norm_cc_g[l])], axis=-1)
        h = h + y @ w_out[l].astype(y.dtype)

        m = rmsnorm(h, norm_mlp_g[l]) @ w_mlp_up[l].astype(h.dtype)
        h = h + jnp.square(jax.nn.relu(m)) @ w_mlp_down[l].astype(h.dtype)

    out = rmsnorm(h, norm_final_g)
    return out[:, N_META:]
```

```python
import math
import numpy as np
import concourse.bass as bass
import concourse.mybir as mybir
from concourse.bass_utils import run_bass_kernel_spmd

F32 = mybir.dt.float32
BF16 = mybir.dt.bfloat16
AF = mybir.ActivationFunctionType
ALU = mybir.AluOpType

D = 1024
KC = 8
NMETA = 16
DFF = 4096
FC = 32
EPS = 1e-6
N = 456
DEBUG = []
GELU_K = math.sqrt(2.0 / math.pi)

G_MIX, G_MLP, G_FIN, G_OUT = 0, 8, 16, 24
C4W, C4B, GAB, GXB, APAR = 32, 48, 52, 56, 60
C31W, C31B, LNG, LNB, IDENT = 64, 188, 192, 196, 200
NPP = 264
HC, HGAB, HGXB, HLNG, HLNB, CEPS, CEPS4, CLN05, CONE, DV_N = 0, 4, 8, 12, 16, 20, 21, 22, 23, 24


class _Eng:
    def __init__(self, name, eng, sem, kind):
        self.name, self.eng, self.sem, self.kind = name, eng, sem, kind
        self.count = 0
        self.known = {}
        self.snaps = {}


class _Buf:
    __slots__ = ("name", "w", "r")

    def __init__(self, name):
        self.name = name
        self.w = None
        self.r = []


class Sched:
    def __init__(self, nc, stack):
        self.nc = nc
        self.stack = stack
        self.sems = {}
        self.latest = {}
        self.E = {}
        for name, eng, kind in (("pe", nc.tensor, "pe"), ("act", nc.scalar, "c"),
                                ("dve", nc.vector, "c"), ("pool", nc.gpsimd, "c"),
                                ("sp", nc.sync, "q")):
            sem = stack.enter_context(nc.semaphore("s_" + name))
            self.sems[name] = sem
            self.latest[name] = 0
            self.E[name] = _Eng(name, eng, sem, kind)
        self.nwaits = 0

    def dma_sem(self, name):
        sem = self.stack.enter_context(self.nc.semaphore("d_" + name))
        self.sems[name] = sem
        self.latest[name] = 0
        return name

    def _need(self, e, reads, writes, need):
        for b in reads:
            if b.w is not None:
                self._add(e, b.w, need, False)
        for b in writes:
            if b.w is not None:
                self._add(e, b.w, need, False)
            for ev in b.r:
                self._add(e, ev, need, True)

    def _add(self, e, ev, need, war):
        key, val, src = ev
        if src == e.name and e.kind == "pe":
            return
        if val > need.get(key, 0):
            need[key] = val

    def _emit_waits(self, e, need):
        for key, val in need.items():
            if val > e.known.get(key, 0):
                e.eng.wait_ge(self.sems[key], val)
                self.nwaits += 1
                e.known[key] = val
                src = self.E.get(key)
                if src is not None and src is not e:
                    snap = src.snaps.get(val)
                    if snap:
                        for k2, v2 in snap.items():
                            if v2 > e.known.get(k2, 0):
                                e.known[k2] = v2

    def _commit(self, ev, reads, writes):
        for b in writes:
            b.w = ev
            b.r = []
        for b in reads:
            if b in writes:
                continue
            b.r = [x for x in b.r if not (x[0] == ev[0])] + [ev]

    def op(self, engname, fn, reads=(), writes=()):
        e = self.E[engname]
        need = {}
        self._need(e, reads, writes, need)
        self._emit_waits(e, need)
        ins = fn(e.eng)
        ins.then_inc(e.sem, 1)
        e.count += 1
        self.latest[engname] = e.count
        e.snaps[e.count] = dict(e.known)
        self._commit((engname, e.count, engname), reads, writes)

    def pe_group(self, items, writes):
        e = self.E["pe"]
        need = {}
        self._need(e, (), writes, need)
        self._emit_waits(e, need)
        allreads = []
        ins = None
        for fn, reads in items:
            need = {}
            self._need(e, reads, (), need)
            self._emit_waits(e, need)
            ins = fn(e.eng)
            for b in reads:
                if b not in allreads:
                    allreads.append(b)
        ins.then_inc(e.sem, 1)
        e.count += 1
        self.latest["pe"] = e.count
        e.snaps[e.count] = dict(e.known)
        self._commit(("pe", e.count, "pe"), allreads, writes)

    def pe_begin(self, writes):
        e = self.E["pe"]
        need = {}
        self._need(e, (), writes, need)
        self._emit_waits(e, need)
        return {"writes": writes, "reads": []}

    def pe_item(self, g, fn, reads, last=False):
        e = self.E["pe"]
        need = {}
        self._need(e, reads, (), need)
        self._emit_waits(e, need)
        ins = fn(e.eng)
        for b in reads:
            if b not in g["reads"]:
                g["reads"].append(b)
        ins.then_inc(e.sem, 1)
        e.count += 1
        self.latest["pe"] = e.count
        e.snaps[e.count] = dict(e.known)
        ev = ("pe", e.count, "pe")
        if last:
            self._commit(ev, g["reads"], g["writes"])
        else:
            for b in reads:
                b.r = [x for x in b.r if not (x[0] == "pe")] + [ev]

    def dma(self, qname, semname, out, in_, reads=(), writes=(), **kw):
        e = self.E[qname]
        need = {}
        for b in reads:
            if b.w is not None and b.w[1] > need.get(b.w[0], 0):
                need[b.w[0]] = b.w[1]
        for b in writes:
            if b.w is not None and b.w[1] > need.get(b.w[0], 0):
                need[b.w[0]] = b.w[1]
            for ev in b.r:
                if ev[1] > need.get(ev[0], 0):
                    need[ev[0]] = ev[1]
        self._emit_waits(e, need)
        e.eng.dma_start(out=out, in_=in_, **kw).then_inc(self.sems[semname], 16)
        self.latest[semname] += 16
        self._commit((semname, self.latest[semname], "dma:" + qname), reads, writes)

    def barrier(self):
        for e in self.E.values():
            need = {k: v for k, v in self.latest.items() if v > 0 and k != e.name}
            self._emit_waits(e, need)


def _build(nc, n_tiles):
    from contextlib import ExitStack
    T = n_tiles * N
    TOUT = T - NMETA

    hT = nc.dram_tensor("hT", [D, T], F32, kind="ExternalInput").ap()
    w_in = nc.dram_tensor("w_in", [D, 2 * D], F32, kind="ExternalInput").ap()
    w_out = nc.dram_tensor("w_out", [D, D], F32, kind="ExternalInput").ap()
    w_up = nc.dram_tensor("w_up", [D, DFF], F32, kind="ExternalInput").ap()
    w_down = nc.dram_tensor("w_down", [DFF, D], F32, kind="ExternalInput").ap()
    gw = nc.dram_tensor("gw", [128, 8, 128], F32, kind="ExternalInput").ap()
    pp = nc.dram_tensor("pp", [128, NPP], F32, kind="ExternalInput").ap()
    outT = nc.dram_tensor("outT", [D, TOUT], F32, kind="ExternalOutput").ap()
    dbg = nc.dram_tensor("dbg", [len(DEBUG), 128, N], F32, kind="ExternalOutput").ap() if DEBUG else None
    h1s = nc.dram_tensor("h1s", [D, T], F32, kind="Internal").ap()
    wub = nc.dram_tensor("wub", [D, DFF], BF16, kind="Internal").ap()
    wdb = nc.dram_tensor("wdb", [DFF, D], BF16, kind="Internal").ap()

    hT_v = hT.rearrange("(kc p) n -> p kc n", p=128)
    h1_v = h1s.rearrange("(kc p) n -> p kc n", p=128)
    out_v = outT.rearrange("(kc p) n -> p kc n", p=128)

    with ExitStack() as top:
        S = Sched(nc, top)

        def sb(name, shape, dt, stack=top):
            return stack.enter_context(nc.sbuf_tensor(name, shape, dt))

        hbuf = [sb("h%d" % i, [128, KC, N], F32) for i in range(3)]
        Bh = [_Buf("h%d" % i) for i in range(3)]
        pp_sb = sb("pp_sb", [128, NPP], F32)
        dv = sb("dv", [128, DV_N], F32)
        ones1024 = sb("ones1024", [128, 128], BF16)
        psum = [top.enter_context(nc.psum_tensor("ps%d" % i, [128, 512], F32)) for i in range(8)]
        Bps = [_Buf("ps%d" % i) for i in range(8)]
        Bconst = _Buf("const")
        sem_hld = [S.dma_sem("hld%d" % i) for i in range(3)]
        sem_hst = [S.dma_sem("hst%d" % i) for i in range(3)]
        sem_w = S.dma_sem("wld")
        sem_c = S.dma_sem("cld")

        sem_dbg = S.dma_sem("dbg") if DEBUG else None

        def dump(name, ap, bufs, t):
            if DEBUG and name in DEBUG and t == 0:
                S.dma("pool", sem_dbg, dbg[DEBUG.index(name)], ap, reads=bufs)

        def col(c, n=1):
            return pp_sb[:, c:c + n]

        def dcol(c, n=1):
            return dv[:, c:c + n]

        with ExitStack() as pm:
            w_in_sb = sb("w_in_sb", [128, KC, 2 * D], BF16, pm)
            w_out_sb = sb("w_out_sb", [128, KC, D], BF16, pm)
            gw_sb = sb("gw_sb", [128, 8, 128], BF16, pm)
            d4 = sb("d4", [128, 16, 128], BF16, pm)
            d31 = sb("d31", [128, 124, 64], BF16, pm)
            ones512 = sb("ones512", [128, 128], BF16, pm)
            u = sb("u", [128, KC, N], BF16, pm)
            sq = sb("sq", [128, 8, N], BF16, pm)
            yn = sb("yn", [128, KC, N], BF16, pm)
            scr = sb("scr", [128, 5, N], F32, pm)
            xl_bf = sb("xl_bf", [128, 4, N + 3], BF16, pm)
            xc32 = sb("xc32", [128, 4, N], F32, pm)
            xcb = sb("xcb", [128, 2, N], BF16, pm)
            rbuf = sb("rbuf", [128, 4, N], F32, pm)
            ibuf = sb("ibuf", [128, 4, N], F32, pm)
            abuf = sb("abuf", [128, 4, N], F32, pm)
            hs = sb("hs", [128, 4, N], F32, pm)
            gl = sb("gl", [128, 4, N], F32, pm)
            tg = sb("tg", [128, 2, N], F32, pm)
            c_bf = sb("c_bf", [128, 4, N + 30], BF16, pm)
            cc32 = sb("cc32", [128, 4, N], F32, pm)
            carry = sb("carry", [128, 4], F32, pm)
            print("SBUF remaining after phase M alloc:", nc.sbuf_bytes_remaining)

            Bu = [_Buf("u%d" % i) for i in range(KC)]
            Bsq = [_Buf("sq%d" % i) for i in range(8)]
            Bscr = [_Buf("scr%d" % i) for i in range(5)]
            Bxl = [_Buf("xl%d" % i) for i in range(4)]
            Bxlh = [_Buf("xlh%d" % i) for i in range(4)]
            Bxc = [_Buf("xc%d" % i) for i in range(4)]
            Bxcb = [_Buf("xcb%d" % i) for i in range(4)]
            Br = [_Buf("r%d" % i) for i in range(4)]
            Bi = [_Buf("i%d" % i) for i in range(4)]
            Ba = [_Buf("a%d" % i) for i in range(4)]
            Bhs = [_Buf("hs%d" % i) for i in range(4)]
            Bgl = [_Buf("gl%d" % i) for i in range(4)]
            Byn = [_Buf("yn%d" % i) for i in range(KC)]
            Btg = [_Buf("tg%d" % i) for i in range(2)]
            Bc = [_Buf("c%d" % i) for i in range(4)]
            Bch = [_Buf("ch%d" % i) for i in range(4)]
            Bcc = [_Buf("cc%d" % i) for i in range(4)]
            tz = rbuf
            Btz = [Br[0], Br[1]]
            Bcarry = [_Buf("carry%d" % i) for i in range(4)]

            S.dma("sp", sem_c, pp_sb[:], pp, writes=[Bconst])
            sem_gw = S.dma_sem("gwld")
            sem_wout = S.dma_sem("woutld")
            Bgw, Bw_out, Bd4, Bd31 = (_Buf("gw"), _Buf("w_out"), _Buf("d4"), _Buf("d31"))
            w_in_v = w_in.rearrange("(kc p) o -> p kc o", p=128)
            w_out_v = w_out.rearrange("(kc p) o -> p kc o", p=128)
            Bw_in = [_Buf("w_in%d" % i) for i in range(4)]
            sem_wi = [S.dma_sem("wi%d" % i) for i in range(4)]
            for q in (0, 3, 2, 1):
                S.dma("pool", sem_wi[q], w_in_sb[:, :, q * 512:(q + 1) * 512], w_in_v[:, :, q * 512:(q + 1) * 512],
                      writes=[Bw_in[q]])
            S.dma("pool", sem_gw, gw_sb[:], gw, writes=[Bgw])
            for q in range(2):
                S.dma("pool", sem_wout, w_out_sb[:, 4 * q:4 * q + 4, :], w_out_v[:, 4 * q:4 * q + 4, :],
                      writes=[_Buf("wp")])
            Bw_out.w = (sem_wout, S.latest[sem_wout], "dma:pool")
            S.dma("sp", sem_hld[0], hbuf[0][:], hT_v[:, :, 0:N], writes=[Bh[0]])

            S.op("dve", lambda e: e.memset(ones1024[:], 1.0 / 1024.0), writes=[Bconst])
            S.op("dve", lambda e: e.memset(ones512[:], 1.0 / 512.0), writes=[Bconst])
            S.op("dve", lambda e: e.memset(dcol(CEPS), EPS), writes=[Bconst])
            S.op("dve", lambda e: e.memset(dcol(CEPS4), 4.0 * EPS), writes=[Bconst])
            S.op("dve", lambda e: e.memset(dcol(CLN05), math.log(0.5)), writes=[Bconst])
            S.op("dve", lambda e: e.memset(dcol(CONE), 1.0), writes=[Bconst])
            S.op("dve", lambda e: e.memset(carry[:], 0.0), writes=Bcarry)
            for j in range(4):
                S.op("dve", lambda e, j=j: e.memset(xl_bf[:, j, 0:3], 0.0), writes=[Bxlh[j]])
                S.op("dve", lambda e, j=j: e.memset(c_bf[:, j, 0:30], 0.0), writes=[Bch[j]])
            S.op("act", lambda e: e.activation(out=dcol(HC, 4), in_=col(APAR, 4), func=AF.Exp, scale=-1.0),
                 reads=[Bconst], writes=[Bconst])
            ecol = dcol(HC, 4)
            tcol = dcol(HGAB, 4)
            S.op("dve", lambda e: e.tensor_scalar(out=tcol, in0=ecol, scalar1=-1.0 / 5.0, scalar2=1.0 / 4.0,
                                                  op0=ALU.mult, op1=ALU.add), reads=[Bconst], writes=[Bconst])
            for cst in (-1.0 / 3.0, 1.0 / 2.0, -1.0):
                S.op("dve", lambda e: e.tensor_tensor(out=tcol, in0=tcol, in1=ecol, op=ALU.mult),
                     reads=[Bconst], writes=[Bconst])
                S.op("dve", lambda e, cst=cst: e.tensor_scalar(out=tcol, in0=tcol, scalar1=-1.0, scalar2=abs(cst),
                                                               op0=ALU.mult, op1=ALU.add),
                     reads=[Bconst], writes=[Bconst])
            S.op("dve", lambda e: e.tensor_tensor(out=ecol, in0=tcol, in1=ecol, op=ALU.mult),
                 reads=[Bconst], writes=[Bconst])
            S.op("dve", lambda e: e.tensor_scalar(out=ecol, in0=ecol, scalar1=-4.0, scalar2=None, op0=ALU.mult),
                 reads=[Bconst], writes=[Bconst])
            for dst, src in ((HGAB, GAB), (HGXB, GXB), (HLNG, LNG), (HLNB, LNB)):
                S.op("dve", lambda e, dst=dst, src=src: e.tensor_scalar(
                    out=dcol(dst, 4), in0=col(src, 4), scalar1=0.5, scalar2=None, op0=ALU.mult),
                    reads=[Bconst], writes=[Bconst])
            pph = pp_sb[:].tensor

            def bc(off, steps):
                return bass.AP(pph, off, [[NPP, 128]] + steps)

            S.op("dve", lambda e: e.memset(d4[:], 0.0), writes=[Bd4])
            for hb in (0, 64):
                S.op("dve", lambda e, hb=hb: e.tensor_tensor(
                    out=d4[hb:hb + 64, :, hb:hb + 64],
                    in0=bass.AP(pph, hb * NPP + IDENT, [[NPP, 64], [0, 16], [1, 64]]),
                    in1=bass.AP(pph, hb * NPP + C4W, [[NPP, 64], [1, 16], [0, 64]]), op=ALU.mult),
                    reads=[Bconst], writes=[Bd4])
            for j in range(4):
                S.op("dve", lambda e, j=j: e.scalar_tensor_tensor(
                    out=d31[:, j * 31:(j + 1) * 31, :], in0=bc(IDENT, [[0, 31], [1, 64]]), scalar=0.5,
                    in1=bc(C31W + 31 * j, [[1, 31], [0, 64]]), op0=ALU.mult, op1=ALU.mult),
                    reads=[Bconst], writes=[Bd31])

            ps_rr = [0]

            def next_ps():
                b = ps_rr[0] % 7
                ps_rr[0] += 1
                return b

            ln_banks = {}

            def mm(out_ap, lhsT, rhs, start, stop):
                return lambda e: e.matmul(out_ap, lhsT, rhs, start=start, stop=stop)

            def mm2(out_ap, lhsT, rhs, start, stop, base):
                return lambda e: e.matmul(out_ap, lhsT, rhs, start=start, stop=stop, tile_position=(base, base))

            def rstd_from(ps_b, epscol, si):
                S.op("act", lambda e: e.activation(out=scr[:, si, :], in_=psum[ps_b][:, 0:N], func=AF.Ln,
                                                   bias=dcol(epscol), scale=1.0),
                     reads=[Bps[ps_b], Bconst], writes=[Bscr[si]])
                S.op("act", lambda e: e.activation(out=scr[:, si, :], in_=scr[:, si, :], func=AF.Exp,
                                                   scale=-0.5),
                     reads=[Bscr[si]], writes=[Bscr[si]])

            sq_rr = [0]

            def next_sq():
                b = sq_rr[0] % 8
                sq_rr[0] += 1
                return b

            def norm_sq(src_aps, src_bufs, scratch=None):
                out = []
                for i in range(len(src_aps)):
                    if scratch is None:
                        s = next_sq()
                        dst, dbuf = sq[:, s, :], Bsq[s]
                    else:
                        dst, dbuf = scratch[i]
                    S.op("act", lambda e, i=i, dst=dst: e.activation(out=dst, in_=src_aps[i], func=AF.Square),
                         reads=[src_bufs[i]], writes=[dbuf])
                    out.append((dst, dbuf))
                return out

            def norm_mm(sqs, ones_t, ps_b):
                n = len(sqs)
                S.pe_group([(mm(psum[ps_b][:, 0:N], ones_t[:], sqs[i][0], i == 0, i == n - 1), [sqs[i][1], Bconst])
                            for i in range(n)], [Bps[ps_b]])

            def win_group(oc):
                pb = next_ps()
                items = [(mm(psum[pb][:, 0:N], w_in_sb[:, kc, oc * 128:(oc + 1) * 128], u[:, kc, :],
                             kc == 0, kc == KC - 1), [Bu[kc], Bw_in[oc // 4]]) for kc in range(KC)]
                S.pe_group(items, [Bps[pb]])
                return pb

            p1_state = {}

            def P1_sq(t):
                b = t % 3
                h = hbuf[b]
                p1_state[t] = norm_sq([h[:, kc, :] for kc in range(KC)], [Bh[b]] * KC,
                                      scratch=[(u[:, kc, :], Bu[kc]) for kc in range(KC)])

            def P1(t):
                b = t % 3
                h = hbuf[b]
                if t not in p1_state:
                    P1_sq(t)
                psb = next_ps()
                norm_mm(p1_state.pop(t), ones1024, psb)
                rstd_from(psb, CEPS, 0)
                for kc in range(KC):
                    S.op("dve", lambda e, kc=kc: e.scalar_tensor_tensor(
                        out=u[:, kc, :], in0=h[:, kc, :], scalar=col(G_MIX + kc), in1=scr[:, 0, :],
                        op0=ALU.mult, op1=ALU.mult), reads=[Bh[b], Bscr[0], Bconst], writes=[Bu[kc]])

            def W_xl(j):
                pb = win_group(j)
                S.op("act", lambda e: e.activation(out=xl_bf[:, j, 3:3 + N], in_=psum[pb][:, 0:N], func=AF.Copy),
                     reads=[Bps[pb]], writes=[Bxl[j]])

            def W_cc(j):
                pgc = win_group(12 + j)
                tgi = j % 2
                S.op("act", lambda e: e.activation(out=tg[:, tgi, :], in_=psum[pgc][:, 0:N], func=AF.Tanh, scale=0.5),
                     reads=[Bps[pgc]], writes=[Btg[tgi]])
                pv = win_group(8 + j)
                S.op("dve", lambda e: e.scalar_tensor_tensor(
                    out=c_bf[:, j, 30:30 + N], in0=tg[:, tgi, :], scalar=1.0, in1=psum[pv][:, 0:N],
                    op0=ALU.add, op1=ALU.mult), reads=[Btg[tgi], Bps[pv]], writes=[Bc[j]])

            def W_gl(j):
                pgl = win_group(4 + j)
                S.op("act", lambda e: e.activation(out=gl[:, j, :], in_=psum[pgl][:, 0:N], func=AF.Gelu_apprx_tanh),
                     reads=[Bps[pgl]], writes=[Bgl[j]])

            def conv4_unit(j):
                pc = next_ps()
                items = [(mm(psum[pc][:, 0:N], d4[:, j * 4 + k, :], xl_bf[:, j, k:k + N], k == 0, k == 3),
                          [Bxl[j], Bxlh[j], Bd4]) for k in range(4)]
                S.pe_group(items, [Bps[pc]])
                S.op("act", lambda e: e.activation(out=xcb[:, j % 2, :], in_=psum[pc][:, 0:N],
                                                   func=AF.Identity, bias=col(C4B + j), scale=1.0),
                     reads=[Bps[pc], Bconst], writes=[Bxcb[j % 2]])
                S.op("act", lambda e: e.activation(out=xc32[:, j, :], in_=psum[pc][:, 0:N],
                                                   func=AF.Identity, bias=col(C4B + j), scale=1.0),
                     reads=[Bps[pc], Bconst], writes=[Bxc[j]])
                S.op("pool", lambda e: e.tensor_copy(out=xl_bf[:, j, 0:3], in_=xl_bf[:, j, N:N + 3]),
                     reads=[Bxl[j]], writes=[Bxlh[j]])

            def gates_unit(j):
                pga = next_ps()
                S.pe_group([(mm(psum[pga][:, 0:N], gw_sb[:, j, :], xcb[:, j % 2, :], True, True),
                             [Bxcb[j % 2], Bgw])], [Bps[pga]])
                pgx = next_ps()
                S.pe_group([(mm(psum[pgx][:, 0:N], gw_sb[:, 4 + j, :], xcb[:, j % 2, :], True, True),
                             [Bxcb[j % 2], Bgw])], [Bps[pgx]])
                S.op("act", lambda e: e.activation(out=rbuf[:, j, :], in_=psum[pga][:, 0:N],
                                                   func=AF.Tanh, bias=dcol(HGAB + j), scale=0.5),
                     reads=[Bps[pga], Bconst], writes=[Br[j]])
                S.op("act", lambda e: e.activation(out=ibuf[:, j, :], in_=psum[pgx][:, 0:N],
                                                   func=AF.Tanh, bias=dcol(HGXB + j), scale=0.5),
                     reads=[Bps[pgx], Bconst], writes=[Bi[j]])
                S.op("act", lambda e: e.activation(out=abuf[:, j, :], in_=rbuf[:, j, :], func=AF.Exp,
                                                   bias=dcol(HC + j), scale=dcol(HC + j)),
                     reads=[Br[j], Bconst], writes=[Ba[j]])
                S.op("pool", lambda e: e.tensor_tensor(out=rbuf[:, j, :], in0=abuf[:, j, :],
                                                       in1=abuf[:, j, :], op=ALU.mult),
                     reads=[Ba[j]], writes=[Br[j]])
                S.op("dve", lambda e: e.scalar_tensor_tensor(
                    out=ibuf[:, j, :], in0=ibuf[:, j, :], scalar=1.0, in1=xc32[:, j, :],
                    op0=ALU.add, op1=ALU.mult), reads=[Bi[j], Bxc[j]], writes=[Bi[j]])

            def conv31_unit(j):
                pa, pb2 = next_ps(), next_ps()
                items = []
                for k in range(31):
                    items.append((mm2(psum[pa][0:64, 0:N], d31[0:64, j * 31 + k, :], c_bf[0:64, j, k:k + N],
                                      k == 0, k == 30, 0), [Bc[j], Bch[j], Bd31]))
                    items.append((mm2(psum[pb2][64:128, 0:N], d31[64:128, j * 31 + k, :], c_bf[64:128, j, k:k + N],
                                      k == 0, k == 30, 64), [Bc[j], Bch[j], Bd31]))
                S.pe_group(items, [Bps[pa], Bps[pb2]])
                S.op("act", lambda e: e.activation(out=cc32[0:64, j, :], in_=psum[pa][0:64, 0:N],
                                                   func=AF.Identity, bias=pp_sb[0:64, C31B + j:C31B + j + 1], scale=1.0),
                     reads=[Bps[pa], Bconst], writes=[Bcc[j]])
                S.op("act", lambda e: e.activation(out=cc32[64:128, j, :], in_=psum[pb2][64:128, 0:N],
                                                   func=AF.Identity, bias=pp_sb[64:128, C31B + j:C31B + j + 1], scale=1.0),
                     reads=[Bps[pb2], Bconst], writes=[Bcc[j]])
                S.op("pool", lambda e: e.tensor_copy(out=c_bf[:, j, 0:30], in_=c_bf[:, j, N:N + 30]),
                     reads=[Bc[j]], writes=[Bch[j]])
                s1, s2 = 2 * j, 2 * j + 1
                S.op("dve", lambda e: e.tensor_copy(out=sq[:, s1, :], in_=cc32[:, j, :]),
                     reads=[Bcc[j]], writes=[Bsq[s1]])
                S.op("dve", lambda e: e.tensor_tensor(out=sq[:, s2, :], in0=cc32[:, j, :],
                                                      in1=cc32[:, j, :], op=ALU.mult),
                     reads=[Bcc[j]], writes=[Bsq[s2]])
                return None

            def conv31_stats(js):
                return None

            def ln_stat_groups():
                pm_, pq_ = 7, next_ps()
                ln_banks["mean"], ln_banks["msq"] = pm_, pq_
                S.pe_group([(mm(psum[pm_][:, 0:N], ones512[:], sq[:, 2 * j, :], j == 0, j == 3),
                             [Bsq[2 * j], Bconst]) for j in range(4)], [Bps[pm_]])
                S.pe_group([(mm(psum[pq_][:, 0:N], ones512[:], sq[:, 2 * j + 1, :], j == 0, j == 3),
                             [Bsq[2 * j + 1], Bconst]) for j in range(4)], [Bps[pq_]])
                sq_rr[0] = 0

            def lru_tail_act():
                for j in range(4):
                    S.op("act", lambda e, j=j: e.activation(out=rbuf[:, j, :], in_=rbuf[:, j, :], func=AF.Ln,
                                                            bias=dcol(CONE), scale=-1.0),
                         reads=[Br[j], Bconst], writes=[Br[j]])
                for j in range(4):
                    S.op("act", lambda e, j=j: e.activation(out=rbuf[:, j, :], in_=rbuf[:, j, :], func=AF.Exp,
                                                            bias=dcol(CLN05), scale=0.5),
                         reads=[Br[j], Bconst], writes=[Br[j]])

            def lru_tail_dve(j):
                S.op("dve", lambda e: e.tensor_tensor(out=ibuf[:, j, :], in0=ibuf[:, j, :],
                                                      in1=rbuf[:, j, :], op=ALU.mult),
                     reads=[Bi[j], Br[j]], writes=[Bi[j]])
                S.op("dve", lambda e: e.tensor_tensor_scan(
                    out=hs[:, j, :], data0=abuf[:, j, :], data1=ibuf[:, j, :], initial=carry[:, j:j + 1],
                    op0=ALU.mult, op1=ALU.add), reads=[Ba[j], Bi[j], Bcarry[j]], writes=[Bhs[j]])
                S.op("dve", lambda e: e.tensor_copy(out=carry[:, j:j + 1], in_=hs[:, j, N - 1:N]),
                     reads=[Bhs[j]], writes=[Bcarry[j]])
                S.op("dve", lambda e: e.tensor_tensor(out=hs[:, j, :], in0=hs[:, j, :],
                                                      in1=gl[:, j, :], op=ALU.mult),
                     reads=[Bhs[j], Bgl[j]], writes=[Bhs[j]])

            st_state = {}

            def s_lru_sq():
                st_state["lru"] = norm_sq([hs[:, j, :] for j in range(4)], Bhs)

            def s_lru_stats():
                p_lru = next_ps()
                norm_mm(st_state.pop("lru"), ones512, p_lru)
                rstd_from(p_lru, CEPS, 3)

            def s_cc_sq():
                st_state["cc"] = norm_sq([cc32[:, j, :] for j in range(4)], Bcc)

            def s_ln_stats():
                pm_, pq_ = ln_banks["mean"], ln_banks["msq"]
                S.op("act", lambda e: e.activation(out=scr[:, 1, :], in_=psum[pm_][:, 0:N], func=AF.Square),
                     reads=[Bps[pm_]], writes=[Bscr[1]])
                S.op("dve", lambda e: e.tensor_tensor(out=scr[:, 1, :], in0=psum[pq_][:, 0:N], in1=scr[:, 1, :],
                                                      op=ALU.subtract), reads=[Bps[pq_], Bscr[1]], writes=[Bscr[1]])
                S.op("act", lambda e: e.activation(out=scr[:, 1, :], in_=scr[:, 1, :], func=AF.Ln,
                                                   bias=dcol(CEPS), scale=1.0),
                     reads=[Bscr[1], Bconst], writes=[Bscr[1]])
                S.op("act", lambda e: e.activation(out=scr[:, 1, :], in_=scr[:, 1, :], func=AF.Exp, scale=-0.5),
                     reads=[Bscr[1]], writes=[Bscr[1]])
                S.op("dve", lambda e: e.tensor_tensor(out=scr[:, 2, :], in0=psum[pm_][:, 0:N], in1=scr[:, 1, :],
                                                      op=ALU.mult), reads=[Bps[pm_], Bscr[1]], writes=[Bscr[2]])

            def s_z(j):
                S.op("dve", lambda e: e.tensor_tensor(out=cc32[:, j, :], in0=cc32[:, j, :], in1=scr[:, 1, :],
                                                      op=ALU.mult), reads=[Bcc[j], Bscr[1]], writes=[Bcc[j]])
                S.op("dve", lambda e: e.tensor_tensor(out=cc32[:, j, :], in0=cc32[:, j, :], in1=scr[:, 2, :],
                                                      op=ALU.subtract), reads=[Bcc[j], Bscr[2]], writes=[Bcc[j]])

            def s_silu(j):
                zi = j % 2
                S.op("act", lambda e: e.activation(out=tz[:, zi, :], in_=cc32[:, j, :], func=AF.Tanh,
                                                   bias=dcol(HLNB + j), scale=dcol(HLNG + j)),
                     reads=[Bcc[j], Bconst], writes=[Btz[zi]])
                S.op("pool", lambda e: e.tensor_scalar(out=cc32[:, j, :], in0=cc32[:, j, :],
                                                       scalar1=col(LNG + j), scalar2=col(LNB + j),
                                                       op0=ALU.mult, op1=ALU.add),
                     reads=[Bcc[j], Bconst], writes=[Bcc[j]])
                S.op("dve", lambda e: e.scalar_tensor_tensor(
                    out=cc32[:, j, :], in0=tz[:, zi, :], scalar=1.0, in1=cc32[:, j, :],
                    op0=ALU.add, op1=ALU.mult), reads=[Btz[zi], Bcc[j]], writes=[Bcc[j]])

            def s_cc_stats():
                p_cc = next_ps()
                norm_mm(st_state.pop("cc"), ones512, p_cc)
                rstd_from(p_cc, CEPS4, 4)

            def s_yn_lru():
                for j in range(4):
                    S.op("dve", lambda e, j=j: e.scalar_tensor_tensor(
                        out=yn[:, j, :], in0=hs[:, j, :], scalar=col(G_OUT + j), in1=scr[:, 3, :],
                        op0=ALU.mult, op1=ALU.mult), reads=[Bhs[j], Bscr[3], Bconst], writes=[Byn[j]])

            def s_yn_cc():
                for j in range(4):
                    S.op("dve", lambda e, j=j: e.scalar_tensor_tensor(
                        out=yn[:, 4 + j, :], in0=cc32[:, j, :], scalar=col(G_OUT + 4 + j), in1=scr[:, 4, :],
                        op0=ALU.mult, op1=ALU.mult), reads=[Bcc[j], Bscr[4], Bconst], writes=[Byn[4 + j]])

            def wout_unit(t, oc):
                b = t % 3
                h = hbuf[b]
                pb = next_ps()
                items = [(mm(psum[pb][:, 0:N], w_out_sb[:, kc, oc * 128:(oc + 1) * 128], yn[:, kc, :],
                             kc == 0, kc == KC - 1), [Byn[kc], Bw_out]) for kc in range(KC)]
                S.pe_group(items, [Bps[pb]])
                S.op("dve", lambda e: e.tensor_tensor(out=h[:, oc, :], in0=h[:, oc, :],
                                                      in1=psum[pb][:, 0:N], op=ALU.add),
                     reads=[Bh[b], Bps[pb]], writes=[Bh[b]])

            def finish_tile(t):
                b = t % 3
                if t == n_tiles - 1:
                    return
                S.dma("sp", sem_hst[b], h1_v[:, :, t * N:(t + 1) * N], hbuf[b][:], reads=[Bh[b]])

            if n_tiles > 1:
                S.dma("sp", sem_hld[1], hbuf[1][:], hT_v[:, :, N:2 * N], writes=[Bh[1]])
            P1(0)
            for j in range(4):
                W_xl(j)
            for j in range(4):
                W_cc(j)
            for j in range(4):
                W_gl(j)
            sem_cv = S.dma_sem("wconv")
            Bwub, Bwdb = _Buf("wub"), _Buf("wdb")
            conv_jobs = []
            for q in range(8):
                conv_jobs.append((wub[q * 128:(q + 1) * 128, :], w_up[q * 128:(q + 1) * 128, :]))
            for q in range(8):
                conv_jobs.append((wdb[q * 512:(q + 1) * 512, :], w_down[q * 512:(q + 1) * 512, :]))
            n_conv_tiles = max(1, min(4, n_tiles - 1))

            def emit_conv(tt):
                per = (len(conv_jobs) + n_conv_tiles - 1) // n_conv_tiles
                for o_ap, i_ap in conv_jobs[tt * per:(tt + 1) * per]:
                    S.dma("pool", sem_cv, o_ap, i_ap, writes=[_Buf("wp")])

            for t in range(n_tiles):
                nxt = t + 1 < n_tiles
                if t < n_conv_tiles:
                    emit_conv(t)
                if t >= 1 and nxt:
                    bb = (t + 1) % 3
                    S.dma("sp", sem_hld[bb], hbuf[bb][:], hT_v[:, :, (t + 1) * N:(t + 2) * N], writes=[Bh[bb]])
                fill = [(lambda oc=oc: wout_unit(t - 1, oc)) for oc in range(KC)] if t > 0 else []
                conv4_unit(0)
                conv4_unit(1)
                gates_unit(0)
                for j in range(2, 4):
                    if fill:
                        fill.pop(0)()
                    conv4_unit(j)
                    if fill:
                        fill.pop(0)()
                    gates_unit(j - 1)
                    if fill:
                        fill.pop(0)()
                while fill:
                    fill.pop(0)()
                if t > 0:
                    finish_tile(t - 1)
                js0 = conv31_unit(0)
                gates_unit(3)
                if nxt:
                    P1_sq(t + 1)
                js1 = conv31_unit(1)
                conv31_stats(js0)
                js2 = conv31_unit(2)
                conv31_stats(js1)
                if nxt:
                    P1(t + 1)
                js3 = conv31_unit(3)
                conv31_stats(js2)
                lru_tail_act()
                if nxt:
                    W_xl(0)
                    W_xl(1)
                ln_stat_groups()
                if nxt:
                    W_xl(2)
                    W_xl(3)
                for j in range(4):
                    lru_tail_dve(j)
                s_ln_stats()
                for j in range(4):
                    s_z(j)
                for j in range(4):
                    if nxt:
                        W_cc(j)
                    s_silu(j)
                if nxt:
                    W_gl(0)
                    W_gl(1)
                s_lru_sq()
                if nxt:
                    W_gl(2)
                    W_gl(3)
                s_lru_stats()
                s_cc_sq()
                s_cc_stats()
                s_yn_lru()
                s_yn_cc()
            for oc in range(KC):
                wout_unit(n_tiles - 1, oc)
            finish_tile(n_tiles - 1)

            Bwub.w = (sem_cv, S.latest[sem_cv], "dma:pool")
            Bwdb.w = Bwub.w
            S.barrier()
        with ExitStack() as pf:
            w_up_sb = sb("w_up_sb", [128, KC, DFF], BF16, pf)
            w_dn_sb = sb("w_dn_sb", [128, FC, D], BF16, pf)
            u2 = sb("u2", [128, KC, N], BF16, pf)
            sqf = sb("sqf", [128, 4, N], BF16, pf)
            scrf = sb("scrf", [128, 2, N], F32, pf)
            act = sb("act", [128, 16, N], BF16, pf)
            rl = sb("rl", [128, 4, N], BF16, pf)
            print("SBUF remaining after phase F alloc:", nc.sbuf_bytes_remaining)
            Bu2 = [_Buf("u2%d" % i) for i in range(KC)]
            Bsqf = [_Buf("sqf%d" % i) for i in range(4)]
            Bscrf = [_Buf("scrf%d" % i) for i in range(2)]
            Bact = [_Buf("act%d" % i) for i in range(16)]
            Brl = [_Buf("rl%d" % i) for i in range(4)]
            Bwu = [_Buf("wup%d" % i) for i in range(4)]
            Bwd = [_Buf("wdn%d" % i) for i in range(4)]
            sem_wu = [S.dma_sem("wup%d" % i) for i in range(4)]
            sem_wd = [S.dma_sem("wdn%d" % i) for i in range(4)]
            wub_v = wub.rearrange("(kc p) o -> p kc o", p=128)
            wdb_v = wdb.rearrange("(fc p) o -> p fc o", p=128)

            def ld_wu(q):
                S.dma("sp", sem_wu[q], w_up_sb[:, :, q * 1024:(q + 1) * 1024], wub_v[:, :, q * 1024:(q + 1) * 1024],
                      reads=[Bwub], writes=[Bwu[q]])

            def ld_wd(q):
                S.dma("sp", sem_wd[q], w_dn_sb[:, q * 8:(q + 1) * 8, :], wdb_v[:, q * 8:(q + 1) * 8, :],
                      reads=[Bwdb], writes=[Bwd[q]])

            ld_wu(0)
            ld_wu(1)
            ld_wd(0)
            ld_wd(1)
            ld_wu(2)
            ld_wu(3)
            ld_wd(2)
            ld_wd(3)

            psf_rr = [0]

            def next_psf():
                b = psf_rr[0] % 8
                psf_rr[0] += 1
                return b

            sqf_rr = [0]
            rl_rr = [0]

            def normF_A(h, Bhb):
                pb = next_psf()
                g = S.pe_begin([Bps[pb]])
                for kc in range(KC):
                    s = sqf_rr[0] % 4
                    sqf_rr[0] += 1
                    S.op("act", lambda e, kc=kc, s=s: e.activation(out=sqf[:, s, :], in_=h[:, kc, :], func=AF.Square),
                         reads=[Bhb], writes=[Bsqf[s]])
                    S.pe_item(g, mm(psum[pb][:, 0:N], ones1024[:], sqf[:, s, :], kc == 0, kc == KC - 1),
                              [Bsqf[s], Bconst], last=(kc == KC - 1))
                return pb

            def normF_B(pb, h, Bhb, gcol0, dst_fn, dst_bufs):
                S.op("act", lambda e: e.activation(out=scrf[:, 0, :], in_=psum[pb][:, 0:N], func=AF.Ln,
                                                   bias=dcol(CEPS), scale=1.0),
                     reads=[Bps[pb], Bconst], writes=[Bscrf[0]])
                S.op("act", lambda e: e.activation(out=scrf[:, 1, :], in_=scrf[:, 0, :], func=AF.Exp, scale=-0.5),
                     reads=[Bscrf[0]], writes=[Bscrf[1]])
                for kc in range(KC):
                    S.op("dve", lambda e, kc=kc: e.scalar_tensor_tensor(
                        out=dst_fn(kc), in0=h[:, kc, :], scalar=col(gcol0 + kc), in1=scrf[:, 1, :],
                        op0=ALU.mult, op1=ALU.mult), reads=[Bhb, Bscrf[1], Bconst], writes=[dst_bufs[kc]])

            perm = [n_tiles - 1] + list(range(n_tiles - 1))
            fbase = (n_tiles - 1) % 3

            def hb(pos):
                return (fbase + pos) % 3

            def S4(pos):
                b = hb(pos)
                pb = normF_A(hbuf[b], Bh[b])
                normF_B(pb, hbuf[b], Bh[b], G_MLP, lambda kc: u2[:, kc, :], Bu2)

            def S7_store(pos):
                b = hb(pos)
                t = perm[pos]
                h = hbuf[b]
                pb = normF_A(h, Bh[b])
                normF_B(pb, h, Bh[b], G_FIN, lambda kc: h[:, kc, :], [Bh[b]] * KC)
                if t == 0:
                    S.dma("sp", sem_hst[b], out_v[:, :, 0:N - NMETA], h[:, :, NMETA:N], reads=[Bh[b]])
                else:
                    S.dma("sp", sem_hst[b], out_v[:, :, t * N - NMETA:(t + 1) * N - NMETA], h[:], reads=[Bh[b]])

            def ld_h(pos):
                b = hb(pos)
                t = perm[pos]
                S.dma("sp", sem_hld[b], hbuf[b][:], h1_v[:, :, t * N:(t + 1) * N], writes=[Bh[b]])

            def up_group(f, fc):
                pb = next_psf()
                items = [(mm(psum[pb][:, 0:N], w_up_sb[:, kc, fc * 128:(fc + 1) * 128], u2[:, kc, :],
                             kc == 0, kc == KC - 1), [Bu2[kc], Bwu[fc // 8]]) for kc in range(KC)]
                S.pe_group(items, [Bps[pb]])
                r = rl_rr[0] % 4
                rl_rr[0] += 1
                S.op("act", lambda e: e.activation(out=rl[:, r, :], in_=psum[pb][:, 0:N], func=AF.Relu),
                     reads=[Bps[pb]], writes=[Brl[r]])
                S.op("pool" if f % 2 else "dve", lambda e: e.tensor_tensor(
                    out=act[:, f, :], in0=rl[:, r, :], in1=rl[:, r, :], op=ALU.mult),
                    reads=[Brl[r]], writes=[Bact[f]])

            def down_group(pos, half, oc):
                b = hb(pos)
                h = hbuf[b]
                pb = next_psf()
                items = [(mm(psum[pb][:, 0:N], w_dn_sb[:, half * 16 + f, oc * 128:(oc + 1) * 128],
                             act[:, f, :], f == 0, f == 15), [Bact[f], Bwd[(half * 16 + f) // 8]]) for f in range(16)]
                S.pe_group(items, [Bps[pb]])
                S.op("dve", lambda e: e.tensor_tensor(out=h[:, oc, :], in0=h[:, oc, :],
                                                      in1=psum[pb][:, 0:N], op=ALU.add),
                     reads=[Bh[b], Bps[pb]], writes=[Bh[b]])

            if n_tiles > 1:
                ld_h(1)
            S4(0)
            for pos in range(n_tiles):
                for half in range(2):
                    for f in range(16):
                        up_group(f, half * 16 + f)
                        if half == 0 and f == 3 and pos > 0:
                            S7_store(pos - 1)
                            if pos + 1 < n_tiles:
                                ld_h(pos + 1)
                    if half == 1 and pos + 1 < n_tiles:
                        S4(pos + 1)
                    for oc in range(KC):
                        down_group(pos, half, oc)
            S7_store(n_tiles - 1)
            S.barrier()
        print("sched waits:", S.nwaits, {k: v for k, v in S.latest.items()})
    return nc


def _host_prep(inp):
    f = np.float32
    x = np.asarray(inp["x"], f)
    B = x.shape[0]
    meta = np.asarray(inp["meta_tokens"], f)
    hTs = []
    for bi in range(B):
        hcat = np.concatenate([meta, x[bi]], axis=0)
        hTs.append(np.ascontiguousarray(hcat.T))
    pp = np.zeros((128, NPP), f)

    def chunked(v, n):
        return np.asarray(v, f).reshape(n, 128).T

    pp[:, G_MIX:G_MIX + 8] = chunked(inp["norm_mix_g"][0], 8)
    pp[:, G_MLP:G_MLP + 8] = chunked(inp["norm_mlp_g"][0], 8)
    pp[:, G_FIN:G_FIN + 8] = chunked(inp["norm_final_g"], 8)
    pp[:, G_OUT:G_OUT + 4] = chunked(inp["out_norm_lru_g"][0], 4)
    pp[:, G_OUT + 4:G_OUT + 8] = chunked(inp["out_norm_cc_g"][0], 4)
    c4 = np.asarray(inp["lru_conv_w"][0], f)
    pp[:, C4W:C4W + 16] = c4.reshape(4, 4, 128).transpose(2, 1, 0).reshape(128, 16)
    pp[:, C4B:C4B + 4] = chunked(inp["lru_conv_b"][0], 4)
    pp[:, GAB:GAB + 4] = chunked(inp["lru_gate_a_b"][0], 4)
    pp[:, GXB:GXB + 4] = chunked(inp["lru_gate_x_b"][0], 4)
    pp[:, APAR:APAR + 4] = chunked(inp["lru_a_param"][0], 4)
    c31 = np.asarray(inp["cc_conv_w"][0], f)
    pp[:, C31W:C31W + 124] = c31.reshape(31, 4, 128).transpose(2, 1, 0).reshape(128, 124)
    pp[:, C31B:C31B + 4] = chunked(inp["cc_conv_b"][0], 4)
    pp[:, LNG:LNG + 4] = chunked(inp["cc_ln_g"][0], 4)
    pp[:, LNB:LNB + 4] = chunked(inp["cc_ln_b"][0], 4)
    pp[:, IDENT:IDENT + 64] = np.concatenate([np.eye(64, dtype=f), np.eye(64, dtype=f)], axis=0)
    gwa = np.asarray(inp["lru_gate_a_w"][0], f)
    gwx = np.asarray(inp["lru_gate_x_w"][0], f)
    gw = np.zeros((128, 8, 128), f)
    for j in range(4):
        for hh in range(2):
            gw[hh * 64:(hh + 1) * 64, j, hh * 64:(hh + 1) * 64] = gwa[2 * j + hh]
            gw[hh * 64:(hh + 1) * 64, 4 + j, hh * 64:(hh + 1) * 64] = gwx[2 * j + hh]
    shared = {
        "w_in": np.ascontiguousarray(inp["w_in"][0], f),
        "w_out": np.ascontiguousarray(inp["w_out"][0], f),
        "w_up": np.ascontiguousarray(inp["w_mlp_up"][0], f),
        "w_down": np.ascontiguousarray(inp["w_mlp_down"][0], f),
        "gw": gw, "pp": pp,
    }
    return hTs, shared


def kernel(**inputs):
    hTs, shared = _host_prep(inputs)
    B = len(hTs)
    T = hTs[0].shape[1]
    assert T % N == 0
    n_tiles = T // N
    nc = bass.Bass("TRN2", target_bir_lowering=False)
    _build(nc, n_tiles)
    in_maps = [dict(shared, hT=hTs[bi]) for bi in range(B)]
    res = run_bass_kernel_spmd(nc, in_maps, core_ids=list(range(B)))
    if DEBUG:
        global LAST_DBG
        LAST_DBG = np.asarray(res.results[0]["dbg"])
    out = np.stack([np.ascontiguousarray(np.asarray(r["outT"]).T) for r in res.results], axis=0)
    return out.astype(np.float32)
```

```python
import math
import numpy as np
import concourse.bass as bass
import concourse.mybir as mybir
from concourse.bass_utils import run_bass_kernel_spmd

F32 = mybir.dt.float32
BF16 = mybir.dt.bfloat16
AF = mybir.ActivationFunctionType
ALU = mybir.AluOpType

D = 1024
KC = 8
NMETA = 16
DFF = 4096
FC = 32
EPS = 1e-6
N = 456
DEBUG = []
GELU_K = math.sqrt(2.0 / math.pi)

G_MIX, G_MLP, G_FIN, G_OUT = 0, 8, 16, 24
C4W, C4B, GAB, GXB, APAR = 32, 48, 52, 56, 60
C31W, C31B, LNG, LNB, IDENT = 64, 188, 192, 196, 200
NPP = 264
HC, HGAB, HGXB, HLNG, HLNB, CEPS, CEPS4, CLN05, CONE, DV_N = 0, 4, 8, 12, 16, 20, 21, 22, 23, 24


class _Eng:
    def __init__(self, name, eng, sem, kind):
        self.name, self.eng, self.sem, self.kind = name, eng, sem, kind
        self.count = 0
        self.known = {}
        self.snaps = {}


class _Buf:
    __slots__ = ("name", "w", "r")

    def __init__(self, name):
        self.name = name
        self.w = None
        self.r = []


class Sched:
    def __init__(self, nc, stack):
        self.nc = nc
        self.stack = stack
        self.sems = {}
        self.latest = {}
        self.E = {}
        for name, eng, kind in (("pe", nc.tensor, "pe"), ("act", nc.scalar, "c"),
                                ("dve", nc.vector, "c"), ("pool", nc.gpsimd, "c"),
                                ("sp", nc.sync, "q")):
            sem = stack.enter_context(nc.semaphore("s_" + name))
            self.sems[name] = sem
            self.latest[name] = 0
            self.E[name] = _Eng(name, eng, sem, kind)
        self.nwaits = 0

    def dma_sem(self, name):
        sem = self.stack.enter_context(self.nc.semaphore("d_" + name))
        self.sems[name] = sem
        self.latest[name] = 0
        return name

    def _need(self, e, reads, writes, need):
        for b in reads:
            if b.w is not None:
                self._add(e, b.w, need, False)
        for b in writes:
            if b.w is not None:
                self._add(e, b.w, need, False)
            for ev in b.r:
                self._add(e, ev, need, True)

    def _add(self, e, ev, need, war):
        key, val, src = ev
        if src == e.name and e.kind == "pe":
            return
        if val > need.get(key, 0):
            need[key] = val

    def _emit_waits(self, e, need):
        for key, val in need.items():
            if val > e.known.get(key, 0):
                e.eng.wait_ge(self.sems[key], val)
                self.nwaits += 1
                e.known[key] = val
                src = self.E.get(key)
                if src is not None and src is not e:
                    snap = src.snaps.get(val)
                    if snap:
                        for k2, v2 in snap.items():
                            if v2 > e.known.get(k2, 0):
                                e.known[k2] = v2

    def _commit(self, ev, reads, writes):
        for b in writes:
            b.w = ev
            b.r = []
        for b in reads:
            if b in writes:
                continue
            b.r = [x for x in b.r if not (x[0] == ev[0])] + [ev]

    def op(self, engname, fn, reads=(), writes=()):
        e = self.E[engname]
        need = {}
        self._need(e, reads, writes, need)
        self._emit_waits(e, need)
        ins = fn(e.eng)
        ins.then_inc(e.sem, 1)
        e.count += 1
        self.latest[engname] = e.count
        e.snaps[e.count] = dict(e.known)
        self._commit((engname, e.count, engname), reads, writes)

    def pe_group(self, items, writes):
        e = self.E["pe"]
        need = {}
        self._need(e, (), writes, need)
        self._emit_waits(e, need)
        allreads = []
        ins = None
        for fn, reads in items:
            need = {}
            self._need(e, reads, (), need)
            self._emit_waits(e, need)
            ins = fn(e.eng)
            for b in reads:
                if b not in allreads:
                    allreads.append(b)
        ins.then_inc(e.sem, 1)
        e.count += 1
        self.latest["pe"] = e.count
        e.snaps[e.count] = dict(e.known)
        self._commit(("pe", e.count, "pe"), allreads, writes)

    def pe_begin(self, writes):
        e = self.E["pe"]
        need = {}
        self._need(e, (), writes, need)
        self._emit_waits(e, need)
        return {"writes": writes, "reads": []}

    def pe_item(self, g, fn, reads, last=False):
        e = self.E["pe"]
        need = {}
        self._need(e, reads, (), need)
        self._emit_waits(e, need)
        ins = fn(e.eng)
        for b in reads:
            if b not in g["reads"]:
                g["reads"].append(b)
        ins.then_inc(e.sem, 1)
        e.count += 1
        self.latest["pe"] = e.count
        e.snaps[e.count] = dict(e.known)
        ev = ("pe", e.count, "pe")
        if last:
            self._commit(ev, g["reads"], g["writes"])
        else:
            for b in reads:
                b.r = [x for x in b.r if not (x[0] == "pe")] + [ev]

    def dma(self, qname, semname, out, in_, reads=(), writes=(), **kw):
        e = self.E[qname]
        need = {}
        for b in reads:
            if b.w is not None and b.w[1] > need.get(b.w[0], 0):
                need[b.w[0]] = b.w[1]
        for b in writes:
            if b.w is not None and b.w[1] > need.get(b.w[0], 0):
                need[b.w[0]] = b.w[1]
            for ev in b.r:
                if ev[1] > need.get(ev[0], 0):
                    need[ev[0]] = ev[1]
        self._emit_waits(e, need)
        e.eng.dma_start(out=out, in_=in_, **kw).then_inc(self.sems[semname], 16)
        self.latest[semname] += 16
        self._commit((semname, self.latest[semname], "dma:" + qname), reads, writes)

    def barrier(self):
        for e in self.E.values():
            need = {k: v for k, v in self.latest.items() if v > 0 and k != e.name}
            self._emit_waits(e, need)


def _build(nc, n_tiles):
    from contextlib import ExitStack
    T = n_tiles * N
    TOUT = T - NMETA

    hT = nc.dram_tensor("hT", [D, T], F32, kind="ExternalInput").ap()
    w_in = nc.dram_tensor("w_in", [D, 2 * D], F32, kind="ExternalInput").ap()
    w_out = nc.dram_tensor("w_out", [D, D], F32, kind="ExternalInput").ap()
    w_up = nc.dram_tensor("w_up", [D, DFF], F32, kind="ExternalInput").ap()
    w_down = nc.dram_tensor("w_down", [DFF, D], F32, kind="ExternalInput").ap()
    gw = nc.dram_tensor("gw", [128, 8, 128], F32, kind="ExternalInput").ap()
    pp = nc.dram_tensor("pp", [128, NPP], F32, kind="ExternalInput").ap()
    outT = nc.dram_tensor("outT", [D, TOUT], F32, kind="ExternalOutput").ap()
    dbg = nc.dram_tensor("dbg", [len(DEBUG), 128, N], F32, kind="ExternalOutput").ap() if DEBUG else None
    h1s = nc.dram_tensor("h1s", [D, T], F32, kind="Internal").ap()
    wub = nc.dram_tensor("wub", [D, DFF], BF16, kind="Internal").ap()
    wdb = nc.dram_tensor("wdb", [DFF, D], BF16, kind="Internal").ap()

    hT_v = hT.rearrange("(kc p) n -> p kc n", p=128)
    h1_v = h1s.rearrange("(kc p) n -> p kc n", p=128)
    out_v = outT.rearrange("(kc p) n -> p kc n", p=128)

    with ExitStack() as top:
        S = Sched(nc, top)

        def sb(name, shape, dt, stack=top):
            return stack.enter_context(nc.sbuf_tensor(name, shape, dt))

        hbuf = [sb("h%d" % i, [128, KC, N], F32) for i in range(3)]
        Bh = [_Buf("h%d" % i) for i in range(3)]
        pp_sb = sb("pp_sb", [128, NPP], F32)
        dv = sb("dv", [128, DV_N], F32)
        ones1024 = sb("ones1024", [128, 128], BF16)
        psum = [top.enter_context(nc.psum_tensor("ps%d" % i, [128, 512], F32)) for i in range(8)]
        Bps = [_Buf("ps%d" % i) for i in range(8)]
        Bconst = _Buf("const")
        sem_hld = [S.dma_sem("hld%d" % i) for i in range(3)]
        sem_hst = [S.dma_sem("hst%d" % i) for i in range(3)]
        sem_w = S.dma_sem("wld")
        sem_c = S.dma_sem("cld")

        sem_dbg = S.dma_sem("dbg") if DEBUG else None

        def dump(name, ap, bufs, t):
            if DEBUG and name in DEBUG and t == 0:
                S.dma("pool", sem_dbg, dbg[DEBUG.index(name)], ap, reads=bufs)

        def col(c, n=1):
            return pp_sb[:, c:c + n]

        def dcol(c, n=1):
            return dv[:, c:c + n]

        with ExitStack() as pm:
            w_in_sb = sb("w_in_sb", [128, KC, 2 * D], BF16, pm)
            w_out_sb = sb("w_out_sb", [128, KC, D], BF16, pm)
            gw_sb = sb("gw_sb", [128, 8, 128], BF16, pm)
            d4 = sb("d4", [128, 16, 128], BF16, pm)
            d31 = sb("d31", [128, 124, 64], BF16, pm)
            ones512 = sb("ones512", [128, 128], BF16, pm)
            u = sb("u", [128, KC, N], BF16, pm)
            sq = sb("sq", [128, 8, N], BF16, pm)
            yn = sb("yn", [128, KC, N], BF16, pm)
            scr = sb("scr", [128, 5, N], F32, pm)
            xl_bf = sb("xl_bf", [128, 4, N + 3], BF16, pm)
            xc32 = sb("xc32", [128, 4, N], F32, pm)
            xcb = sb("xcb", [128, 2, N], BF16, pm)
            rbuf = sb("rbuf", [128, 4, N], F32, pm)
            ibuf = sb("ibuf", [128, 4, N], F32, pm)
            abuf = sb("abuf", [128, 4, N], F32, pm)
            hs = sb("hs", [128, 4, N], F32, pm)
            gl = sb("gl", [128, 4, N], F32, pm)
            tg = sb("tg", [128, 2, N], F32, pm)
            c_bf = sb("c_bf", [128, 4, N + 30], BF16, pm)
            cc32 = sb("cc32", [128, 4, N], F32, pm)
            carry = sb("carry", [128, 4], F32, pm)
            print("SBUF remaining after phase M alloc:", nc.sbuf_bytes_remaining)

            Bu = [_Buf("u%d" % i) for i in range(KC)]
            Bsq = [_Buf("sq%d" % i) for i in range(8)]
            Bscr = [_Buf("scr%d" % i) for i in range(5)]
            Bxl = [_Buf("xl%d" % i) for i in range(4)]
            Bxlh = [_Buf("xlh%d" % i) for i in range(4)]
            Bxc = [_Buf("xc%d" % i) for i in range(4)]
            Bxcb = [_Buf("xcb%d" % i) for i in range(4)]
            Br = [_Buf("r%d" % i) for i in range(4)]
            Bi = [_Buf("i%d" % i) for i in range(4)]
            Ba = [_Buf("a%d" % i) for i in range(4)]
            Bhs = [_Buf("hs%d" % i) for i in range(4)]
            Bgl = [_Buf("gl%d" % i) for i in range(4)]
            Byn = [_Buf("yn%d" % i) for i in range(KC)]
            Btg = [_Buf("tg%d" % i) for i in range(2)]
            Bc = [_Buf("c%d" % i) for i in range(4)]
            Bch = [_Buf("ch%d" % i) for i in range(4)]
            Bcc = [_Buf("cc%d" % i) for i in range(4)]
            tz = rbuf
            Btz = [Br[0], Br[1]]
            Bcarry = [_Buf("carry%d" % i) for i in range(4)]

            S.dma("sp", sem_c, pp_sb[:], pp, writes=[Bconst])
            sem_gw = S.dma_sem("gwld")
            sem_wout = S.dma_sem("woutld")
            Bgw, Bw_out, Bd4, Bd31 = (_Buf("gw"), _Buf("w_out"), _Buf("d4"), _Buf("d31"))
            w_in_v = w_in.rearrange("(kc p) o -> p kc o", p=128)
            w_out_v = w_out.rearrange("(kc p) o -> p kc o", p=128)
            Bw_in = [_Buf("w_in%d" % i) for i in range(4)]
            sem_wi = [S.dma_sem("wi%d" % i) for i in range(4)]
            for q in (0, 3, 2, 1):
                S.dma("pool", sem_wi[q], w_in_sb[:, :, q * 512:(q + 1) * 512], w_in_v[:, :, q * 512:(q + 1) * 512],
                      writes=[Bw_in[q]])
            S.dma("pool", sem_gw, gw_sb[:], gw, writes=[Bgw])
            for q in range(2):
                S.dma("pool", sem_wout, w_out_sb[:, 4 * q:4 * q + 4, :], w_out_v[:, 4 * q:4 * q + 4, :],
                      writes=[_Buf("wp")])
            Bw_out.w = (sem_wout, S.latest[sem_wout], "dma:pool")
            S.dma("sp", sem_hld[0], hbuf[0][:], hT_v[:, :, 0:N], writes=[Bh[0]])

            S.op("dve", lambda e: e.memset(ones1024[:], 1.0 / 1024.0), writes=[Bconst])
            S.op("dve", lambda e: e.memset(ones512[:], 1.0 / 512.0), writes=[Bconst])
            S.op("dve", lambda e: e.memset(dcol(CEPS), EPS), writes=[Bconst])
            S.op("dve", lambda e: e.memset(dcol(CEPS4), 4.0 * EPS), writes=[Bconst])
            S.op("dve", lambda e: e.memset(dcol(CLN05), math.log(0.5)), writes=[Bconst])
            S.op("dve", lambda e: e.memset(dcol(CONE), 1.0), writes=[Bconst])
            S.op("dve", lambda e: e.memset(carry[:], 0.0), writes=Bcarry)
            for j in range(4):
                S.op("dve", lambda e, j=j: e.memset(xl_bf[:, j, 0:3], 0.0), writes=[Bxlh[j]])
                S.op("dve", lambda e, j=j: e.memset(c_bf[:, j, 0:30], 0.0), writes=[Bch[j]])
            S.op("act", lambda e: e.activation(out=dcol(HC, 4), in_=col(APAR, 4), func=AF.Exp, scale=-1.0),
                 reads=[Bconst], writes=[Bconst])
            ecol = dcol(HC, 4)
            tcol = dcol(HGAB, 4)
            S.op("dve", lambda e: e.tensor_scalar(out=tcol, in0=ecol, scalar1=-1.0 / 5.0, scalar2=1.0 / 4.0,
                                                  op0=ALU.mult, op1=ALU.add), reads=[Bconst], writes=[Bconst])
            for cst in (-1.0 / 3.0, 1.0 / 2.0, -1.0):
                S.op("dve", lambda e: e.tensor_tensor(out=tcol, in0=tcol, in1=ecol, op=ALU.mult),
                     reads=[Bconst], writes=[Bconst])
                S.op("dve", lambda e, cst=cst: e.tensor_scalar(out=tcol, in0=tcol, scalar1=-1.0, scalar2=abs(cst),
                                                               op0=ALU.mult, op1=ALU.add),
                     reads=[Bconst], writes=[Bconst])
            S.op("dve", lambda e: e.tensor_tensor(out=ecol, in0=tcol, in1=ecol, op=ALU.mult),
                 reads=[Bconst], writes=[Bconst])
            S.op("dve", lambda e: e.tensor_scalar(out=ecol, in0=ecol, scalar1=-4.0, scalar2=None, op0=ALU.mult),
                 reads=[Bconst], writes=[Bconst])
            for dst, src in ((HGAB, GAB), (HGXB, GXB), (HLNG, LNG), (HLNB, LNB)):
                S.op("dve", lambda e, dst=dst, src=src: e.tensor_scalar(
                    out=dcol(dst, 4), in0=col(src, 4), scalar1=0.5, scalar2=None, op0=ALU.mult),
                    reads=[Bconst], writes=[Bconst])
            pph = pp_sb[:].tensor

            def bc(off, steps):
                return bass.AP(pph, off, [[NPP, 128]] + steps)

            S.op("dve", lambda e: e.memset(d4[:], 0.0), writes=[Bd4])
            for hb in (0, 64):
                S.op("dve", lambda e, hb=hb: e.tensor_tensor(
                    out=d4[hb:hb + 64, :, hb:hb + 64],
                    in0=bass.AP(pph, hb * NPP + IDENT, [[NPP, 64], [0, 16], [1, 64]]),
                    in1=bass.AP(pph, hb * NPP + C4W, [[NPP, 64], [1, 16], [0, 64]]), op=ALU.mult),
                    reads=[Bconst], writes=[Bd4])
            for j in range(4):
                S.op("dve", lambda e, j=j: e.scalar_tensor_tensor(
                    out=d31[:, j * 31:(j + 1) * 31, :], in0=bc(IDENT, [[0, 31], [1, 64]]), scalar=0.5,
                    in1=bc(C31W + 31 * j, [[1, 31], [0, 64]]), op0=ALU.mult, op1=ALU.mult),
                    reads=[Bconst], writes=[Bd31])

            ps_rr = [0]

            def next_ps():
                b = ps_rr[0] % 7
                ps_rr[0] += 1
                return b

            ln_banks = {}

            def mm(out_ap, lhsT, rhs, start, stop):
                return lambda e: e.matmul(out_ap, lhsT, rhs, start=start, stop=stop)

            def mm2(out_ap, lhsT, rhs, start, stop, base):
                return lambda e: e.matmul(out_ap, lhsT, rhs, start=start, stop=stop, tile_position=(base, base))

            def rstd_from(ps_b, epscol, si):
                S.op("act", lambda e: e.activation(out=scr[:, si, :], in_=psum[ps_b][:, 0:N], func=AF.Ln,
                                                   bias=dcol(epscol), scale=1.0),
                     reads=[Bps[ps_b], Bconst], writes=[Bscr[si]])
                S.op("act", lambda e: e.activation(out=scr[:, si, :], in_=scr[:, si, :], func=AF.Exp,
                                                   scale=-0.5),
                     reads=[Bscr[si]], writes=[Bscr[si]])

            sq_rr = [0]

            def next_sq():
                b = sq_rr[0] % 8
                sq_rr[0] += 1
                return b

            def norm_sq(src_aps, src_bufs, scratch=None):
                out = []
                for i in range(len(src_aps)):
                    if scratch is None:
                        s = next_sq()
                        dst, dbuf = sq[:, s, :], Bsq[s]
                    else:
                        dst, dbuf = scratch[i]
                    S.op("act", lambda e, i=i, dst=dst: e.activation(out=dst, in_=src_aps[i], func=AF.Square),
                         reads=[src_bufs[i]], writes=[dbuf])
                    out.append((dst, dbuf))
                return out

            def norm_mm(sqs, ones_t, ps_b):
                n = len(sqs)
                S.pe_group([(mm(psum[ps_b][:, 0:N], ones_t[:], sqs[i][0], i == 0, i == n - 1), [sqs[i][1], Bconst])
                            for i in range(n)], [Bps[ps_b]])

            def win_group(oc):
                pb = next_ps()
                items = [(mm(psum[pb][:, 0:N], w_in_sb[:, kc, oc * 128:(oc + 1) * 128], u[:, kc, :],
                             kc == 0, kc == KC - 1), [Bu[kc], Bw_in[oc // 4]]) for kc in range(KC)]
                S.pe_group(items, [Bps[pb]])
                return pb

            p1_state = {}

            def P1_sq(t):
                b = t % 3
                h = hbuf[b]
                p1_state[t] = norm_sq([h[:, kc, :] for kc in range(KC)], [Bh[b]] * KC,
                                      scratch=[(u[:, kc, :], Bu[kc]) for kc in range(KC)])

            def P1(t):
                b = t % 3
                h = hbuf[b]
                if t not in p1_state:
                    P1_sq(t)
                psb = next_ps()
                norm_mm(p1_state.pop(t), ones1024, psb)
                rstd_from(psb, CEPS, 0)
                for kc in range(KC):
                    S.op("dve", lambda e, kc=kc: e.scalar_tensor_tensor(
                        out=u[:, kc, :], in0=h[:, kc, :], scalar=col(G_MIX + kc), in1=scr[:, 0, :],
                        op0=ALU.mult, op1=ALU.mult), reads=[Bh[b], Bscr[0], Bconst], writes=[Bu[kc]])

            def W_xl(j):
                pb = win_group(j)
                S.op("act", lambda e: e.activation(out=xl_bf[:, j, 3:3 + N], in_=psum[pb][:, 0:N], func=AF.Copy),
                     reads=[Bps[pb]], writes=[Bxl[j]])

            def W_cc(j):
                pgc = win_group(12 + j)
                tgi = j % 2
                S.op("act", lambda e: e.activation(out=tg[:, tgi, :], in_=psum[pgc][:, 0:N], func=AF.Tanh, scale=0.5),
                     reads=[Bps[pgc]], writes=[Btg[tgi]])
                pv = win_group(8 + j)
                S.op("dve", lambda e: e.scalar_tensor_tensor(
                    out=c_bf[:, j, 30:30 + N], in0=tg[:, tgi, :], scalar=1.0, in1=psum[pv][:, 0:N],
                    op0=ALU.add, op1=ALU.mult), reads=[Btg[tgi], Bps[pv]], writes=[Bc[j]])

            def W_gl(j):
                pgl = win_group(4 + j)
                S.op("act", lambda e: e.activation(out=gl[:, j, :], in_=psum[pgl][:, 0:N], func=AF.Gelu_apprx_tanh),
                     reads=[Bps[pgl]], writes=[Bgl[j]])

            def conv4_unit(j):
                pc = next_ps()
                items = [(mm(psum[pc][:, 0:N], d4[:, j * 4 + k, :], xl_bf[:, j, k:k + N], k == 0, k == 3),
                          [Bxl[j], Bxlh[j], Bd4]) for k in range(4)]
                S.pe_group(items, [Bps[pc]])
                S.op("act", lambda e: e.activation(out=xcb[:, j % 2, :], in_=psum[pc][:, 0:N],
                                                   func=AF.Identity, bias=col(C4B + j), scale=1.0),
                     reads=[Bps[pc], Bconst], writes=[Bxcb[j % 2]])
                S.op("act", lambda e: e.activation(out=xc32[:, j, :], in_=psum[pc][:, 0:N],
                                                   func=AF.Identity, bias=col(C4B + j), scale=1.0),
                     reads=[Bps[pc], Bconst], writes=[Bxc[j]])
                S.op("pool", lambda e: e.tensor_copy(out=xl_bf[:, j, 0:3], in_=xl_bf[:, j, N:N + 3]),
                     reads=[Bxl[j]], writes=[Bxlh[j]])

            def gates_unit(j):
                pga = next_ps()
                S.pe_group([(mm(psum[pga][:, 0:N], gw_sb[:, j, :], xcb[:, j % 2, :], True, True),
                             [Bxcb[j % 2], Bgw])], [Bps[pga]])
                pgx = next_ps()
                S.pe_group([(mm(psum[pgx][:, 0:N], gw_sb[:, 4 + j, :], xcb[:, j % 2, :], True, True),
                             [Bxcb[j % 2], Bgw])], [Bps[pgx]])
                S.op("act", lambda e: e.activation(out=rbuf[:, j, :], in_=psum[pga][:, 0:N],
                                                   func=AF.Tanh, bias=dcol(HGAB + j), scale=0.5),
                     reads=[Bps[pga], Bconst], writes=[Br[j]])
                S.op("act", lambda e: e.activation(out=ibuf[:, j, :], in_=psum[pgx][:, 0:N],
                                                   func=AF.Tanh, bias=dcol(HGXB + j), scale=0.5),
                     reads=[Bps[pgx], Bconst], writes=[Bi[j]])
                S.op("act", lambda e: e.activation(out=abuf[:, j, :], in_=rbuf[:, j, :], func=AF.Exp,
                                                   bias=dcol(HC + j), scale=dcol(HC + j)),
                     reads=[Br[j], Bconst], writes=[Ba[j]])
                S.op("pool", lambda e: e.tensor_tensor(out=rbuf[:, j, :], in0=abuf[:, j, :],
                                                       in1=abuf[:, j, :], op=ALU.mult),
                     reads=[Ba[j]], writes=[Br[j]])
                S.op("dve", lambda e: e.scalar_tensor_tensor(
                    out=ibuf[:, j, :], in0=ibuf[:, j, :], scalar=1.0, in1=xc32[:, j, :],
                    op0=ALU.add, op1=ALU.mult), reads=[Bi[j], Bxc[j]], writes=[Bi[j]])

            def conv31_unit(j):
                pa, pb2 = next_ps(), next_ps()
                items = []
                for k in range(31):
                    items.append((mm2(psum[pa][0:64, 0:N], d31[0:64, j * 31 + k, :], c_bf[0:64, j, k:k + N],
                                      k == 0, k == 30, 0), [Bc[j], Bch[j], Bd31]))
                    items.append((mm2(psum[pb2][64:128, 0:N], d31[64:128, j * 31 + k, :], c_bf[64:128, j, k:k + N],
                                      k == 0, k == 30, 64), [Bc[j], Bch[j], Bd31]))
                S.pe_group(items, [Bps[pa], Bps[pb2]])
                S.op("act", lambda e: e.activation(out=cc32[0:64, j, :], in_=psum[pa][0:64, 0:N],
                                                   func=AF.Identity, bias=pp_sb[0:64, C31B + j:C31B + j + 1], scale=1.0),
                     reads=[Bps[pa], Bconst], writes=[Bcc[j]])
                S.op("act", lambda e: e.activation(out=cc32[64:128, j, :], in_=psum[pb2][64:128, 0:N],
                                                   func=AF.Identity, bias=pp_sb[64:128, C31B + j:C31B + j + 1], scale=1.0),
                     reads=[Bps[pb2], Bconst], writes=[Bcc[j]])
                S.op("pool", lambda e: e.tensor_copy(out=c_bf[:, j, 0:30], in_=c_bf[:, j, N:N + 30]),
                     reads=[Bc[j]], writes=[Bch[j]])
                s1, s2 = 2 * j, 2 * j + 1
                S.op("dve", lambda e: e.tensor_copy(out=sq[:, s1, :], in_=cc32[:, j, :]),
                     reads=[Bcc[j]], writes=[Bsq[s1]])
                S.op("dve", lambda e: e.tensor_tensor(out=sq[:, s2, :], in0=cc32[:, j, :],
                                                      in1=cc32[:, j, :], op=ALU.mult),
                     reads=[Bcc[j]], writes=[Bsq[s2]])
                return None

            def conv31_stats(js):
                return None

            def ln_stat_groups():
                pm_, pq_ = 7, next_ps()
                ln_banks["mean"], ln_banks["msq"] = pm_, pq_
                S.pe_group([(mm(psum[pm_][:, 0:N], ones512[:], sq[:, 2 * j, :], j == 0, j == 3),
                             [Bsq[2 * j], Bconst]) for j in range(4)], [Bps[pm_]])
                S.pe_group([(mm(psum[pq_][:, 0:N], ones512[:], sq[:, 2 * j + 1, :], j == 0, j == 3),
                             [Bsq[2 * j + 1], Bconst]) for j in range(4)], [Bps[pq_]])
                sq_rr[0] = 0

            def lru_tail_act():
                for j in range(4):
                    S.op("act", lambda e, j=j: e.activation(out=rbuf[:, j, :], in_=rbuf[:, j, :], func=AF.Ln,
                                                            bias=dcol(CONE), scale=-1.0),
                         reads=[Br[j], Bconst], writes=[Br[j]])
                for j in range(4):
                    S.op("act", lambda e, j=j: e.activation(out=rbuf[:, j, :], in_=rbuf[:, j, :], func=AF.Exp,
                                                            bias=dcol(CLN05), scale=0.5),
                         reads=[Br[j], Bconst], writes=[Br[j]])

            def lru_tail_dve(j):
                S.op("dve", lambda e: e.tensor_tensor(out=ibuf[:, j, :], in0=ibuf[:, j, :],
                                                      in1=rbuf[:, j, :], op=ALU.mult),
                     reads=[Bi[j], Br[j]], writes=[Bi[j]])
                S.op("dve", lambda e: e.tensor_tensor_scan(
                    out=hs[:, j, :], data0=abuf[:, j, :], data1=ibuf[:, j, :], initial=carry[:, j:j + 1],
                    op0=ALU.mult, op1=ALU.add), reads=[Ba[j], Bi[j], Bcarry[j]], writes=[Bhs[j]])
                S.op("dve", lambda e: e.tensor_copy(out=carry[:, j:j + 1], in_=hs[:, j, N - 1:N]),
                     reads=[Bhs[j]], writes=[Bcarry[j]])
                S.op("dve", lambda e: e.tensor_tensor(out=hs[:, j, :], in0=hs[:, j, :],
                                                      in1=gl[:, j, :], op=ALU.mult),
                     reads=[Bhs[j], Bgl[j]], writes=[Bhs[j]])

            st_state = {}

            def s_lru_sq():
                st_state["lru"] = norm_sq([hs[:, j, :] for j in range(4)], Bhs)

            def s_lru_stats():
                p_lru = next_ps()
                norm_mm(st_state.pop("lru"), ones512, p_lru)
                rstd_from(p_lru, CEPS, 3)

            def s_cc_sq():
                st_state["cc"] = norm_sq([cc32[:, j, :] for j in range(4)], Bcc)

            def s_ln_stats():
                pm_, pq_ = ln_banks["mean"], ln_banks["msq"]
                S.op("act", lambda e: e.activation(out=scr[:, 1, :], in_=psum[pm_][:, 0:N], func=AF.Square),
                     reads=[Bps[pm_]], writes=[Bscr[1]])
                S.op("dve", lambda e: e.tensor_tensor(out=scr[:, 1, :], in0=psum[pq_][:, 0:N], in1=scr[:, 1, :],
                                                      op=ALU.subtract), reads=[Bps[pq_], Bscr[1]], writes=[Bscr[1]])
                S.op("act", lambda e: e.activation(out=scr[:, 1, :], in_=scr[:, 1, :], func=AF.Ln,
                                                   bias=dcol(CEPS), scale=1.0),
                     reads=[Bscr[1], Bconst], writes=[Bscr[1]])
                S.op("act", lambda e: e.activation(out=scr[:, 1, :], in_=scr[:, 1, :], func=AF.Exp, scale=-0.5),
                     reads=[Bscr[1]], writes=[Bscr[1]])
                S.op("dve", lambda e: e.tensor_tensor(out=scr[:, 2, :], in0=psum[pm_][:, 0:N], in1=scr[:, 1, :],
                                                      op=ALU.mult), reads=[Bps[pm_], Bscr[1]], writes=[Bscr[2]])

            def s_z(j):
                S.op("dve", lambda e: e.tensor_tensor(out=cc32[:, j, :], in0=cc32[:, j, :], in1=scr[:, 1, :],
                                                      op=ALU.mult), reads=[Bcc[j], Bscr[1]], writes=[Bcc[j]])
                S.op("dve", lambda e: e.tensor_tensor(out=cc32[:, j, :], in0=cc32[:, j, :], in1=scr[:, 2, :],
                                                      op=ALU.subtract), reads=[Bcc[j], Bscr[2]], writes=[Bcc[j]])

            def s_silu(j):
                zi = j % 2
                S.op("act", lambda e: e.activation(out=tz[:, zi, :], in_=cc32[:, j, :], func=AF.Tanh,
                                                   bias=dcol(HLNB + j), scale=dcol(HLNG + j)),
                     reads=[Bcc[j], Bconst], writes=[Btz[zi]])
                S.op("pool", lambda e: e.tensor_scalar(out=cc32[:, j, :], in0=cc32[:, j, :],
                                                       scalar1=col(LNG + j), scalar2=col(LNB + j),
                                                       op0=ALU.mult, op1=ALU.add),
                     reads=[Bcc[j], Bconst], writes=[Bcc[j]])
                S.op("dve", lambda e: e.scalar_tensor_tensor(
                    out=cc32[:, j, :], in0=tz[:, zi, :], scalar=1.0, in1=cc32[:, j, :],
                    op0=ALU.add, op1=ALU.mult), reads=[Btz[zi], Bcc[j]], writes=[Bcc[j]])

            def s_cc_stats():
                p_cc = next_ps()
                norm_mm(st_state.pop("cc"), ones512, p_cc)
                rstd_from(p_cc, CEPS4, 4)

            def s_yn_lru():
                for j in range(4):
                    S.op("dve", lambda e, j=j: e.scalar_tensor_tensor(
                        out=yn[:, j, :], in0=hs[:, j, :], scalar=col(G_OUT + j), in1=scr[:, 3, :],
                        op0=ALU.mult, op1=ALU.mult), reads=[Bhs[j], Bscr[3], Bconst], writes=[Byn[j]])

            def s_yn_cc():
                for j in range(4):
                    S.op("dve", lambda e, j=j: e.scalar_tensor_tensor(
                        out=yn[:, 4 + j, :], in0=cc32[:, j, :], scalar=col(G_OUT + 4 + j), in1=scr[:, 4, :],
                        op0=ALU.mult, op1=ALU.mult), reads=[Bcc[j], Bscr[4], Bconst], writes=[Byn[4 + j]])

            def wout_unit(t, oc):
                b = t % 3
                h = hbuf[b]
                pb = next_ps()
                items = [(mm(psum[pb][:, 0:N], w_out_sb[:, kc, oc * 128:(oc + 1) * 128], yn[:, kc, :],
                             kc == 0, kc == KC - 1), [Byn[kc], Bw_out]) for kc in range(KC)]
                S.pe_group(items, [Bps[pb]])
                S.op("dve", lambda e: e.tensor_tensor(out=h[:, oc, :], in0=h[:, oc, :],
                                                      in1=psum[pb][:, 0:N], op=ALU.add),
                     reads=[Bh[b], Bps[pb]], writes=[Bh[b]])

            def finish_tile(t):
                b = t % 3
                if t == n_tiles - 1:
                    return
                S.dma("sp", sem_hst[b], h1_v[:, :, t * N:(t + 1) * N], hbuf[b][:], reads=[Bh[b]])

            if n_tiles > 1:
                S.dma("sp", sem_hld[1], hbuf[1][:], hT_v[:, :, N:2 * N], writes=[Bh[1]])
            P1(0)
            for j in range(4):
                W_xl(j)
            for j in range(4):
                W_cc(j)
            for j in range(4):
                W_gl(j)
            sem_cv = S.dma_sem("wconv")
            Bwub, Bwdb = _Buf("wub"), _Buf("wdb")
            conv_jobs = []
            for q in range(8):
                conv_jobs.append((wub[q * 128:(q + 1) * 128, :], w_up[q * 128:(q + 1) * 128, :]))
            for q in range(8):
                conv_jobs.append((wdb[q * 512:(q + 1) * 512, :], w_down[q * 512:(q + 1) * 512, :]))
            n_conv_tiles = max(1, min(4, n_tiles - 1))

            def emit_conv(tt):
                per = (len(conv_jobs) + n_conv_tiles - 1) // n_conv_tiles
                for o_ap, i_ap in conv_jobs[tt * per:(tt + 1) * per]:
                    S.dma("pool", sem_cv, o_ap, i_ap, writes=[_Buf("wp")])

            for t in range(n_tiles):
                nxt = t + 1 < n_tiles
                if t < n_conv_tiles:
                    emit_conv(t)
                if t >= 1 and nxt:
                    bb = (t + 1) % 3
                    S.dma("sp", sem_hld[bb], hbuf[bb][:], hT_v[:, :, (t + 1) * N:(t + 2) * N], writes=[Bh[bb]])
                fill = [(lambda oc=oc: wout_unit(t - 1, oc)) for oc in range(KC)] if t > 0 else []
                conv4_unit(0)
                conv4_unit(1)
                gates_unit(0)
                for j in range(2, 4):
                    if fill:
                        fill.pop(0)()
                    conv4_unit(j)
                    if fill:
                        fill.pop(0)()
                    gates_unit(j - 1)
                    if fill:
                        fill.pop(0)()
                while fill:
                    fill.pop(0)()
                if t > 0:
                    finish_tile(t - 1)
                js0 = conv31_unit(0)
                gates_unit(3)
                if nxt:
                    P1_sq(t + 1)
                js1 = conv31_unit(1)
                conv31_stats(js0)
                js2 = conv31_unit(2)
                conv31_stats(js1)
                if nxt:
                    P1(t + 1)
                js3 = conv31_unit(3)
                conv31_stats(js2)
                lru_tail_act()
                if nxt:
                    W_xl(0)
                    W_xl(1)
                ln_stat_groups()
                if nxt:
                    W_xl(2)
                    W_xl(3)
                for j in range(4):
                    lru_tail_dve(j)
                s_ln_stats()
                for j in range(4):
                    s_z(j)
                for j in range(4):
                    if nxt:
                        W_cc(j)
                    s_silu(j)
                if nxt:
                    W_gl(0)
                    W_gl(1)
                s_lru_sq()
                s_cc_sq()
                if nxt:
                    W_gl(2)
                    W_gl(3)
                s_lru_stats()
                s_cc_stats()
                s_yn_lru()
                s_yn_cc()
            for oc in range(KC):
                wout_unit(n_tiles - 1, oc)
            finish_tile(n_tiles - 1)

            Bwub.w = (sem_cv, S.latest[sem_cv], "dma:pool")
            Bwdb.w = Bwub.w
            S.barrier()
        with ExitStack() as pf:
            w_up_sb = sb("w_up_sb", [128, KC, DFF], BF16, pf)
            w_dn_sb = sb("w_dn_sb", [128, FC, D], BF16, pf)
            u2 = sb("u2", [128, KC, N], BF16, pf)
            sqf = sb("sqf", [128, 4, N], BF16, pf)
            scrf = sb("scrf", [128, 2, N], F32, pf)
            act = sb("act", [128, 16, N], BF16, pf)
            rl = sb("rl", [128, 4, N], BF16, pf)
            print("SBUF remaining after phase F alloc:", nc.sbuf_bytes_remaining)
            Bu2 = [_Buf("u2%d" % i) for i in range(KC)]
            Bsqf = [_Buf("sqf%d" % i) for i in range(4)]
            Bscrf = [_Buf("scrf%d" % i) for i in range(2)]
            Bact = [_Buf("act%d" % i) for i in range(16)]
            Brl = [_Buf("rl%d" % i) for i in range(4)]
            Bwu = [_Buf("wup%d" % i) for i in range(4)]
            Bwd = [_Buf("wdn%d" % i) for i in range(4)]
            sem_wu = [S.dma_sem("wup%d" % i) for i in range(4)]
            sem_wd = [S.dma_sem("wdn%d" % i) for i in range(4)]
            wub_v = wub.rearrange("(kc p) o -> p kc o", p=128)
            wdb_v = wdb.rearrange("(fc p) o -> p fc o", p=128)

            def ld_wu(q):
                S.dma("sp", sem_wu[q], w_up_sb[:, :, q * 1024:(q + 1) * 1024], wub_v[:, :, q * 1024:(q + 1) * 1024],
                      reads=[Bwub], writes=[Bwu[q]])

            def ld_wd(q):
                S.dma("sp", sem_wd[q], w_dn_sb[:, q * 8:(q + 1) * 8, :], wdb_v[:, q * 8:(q + 1) * 8, :],
                      reads=[Bwdb], writes=[Bwd[q]])

            ld_wu(0)
            ld_wu(1)
            ld_wd(0)
            ld_wd(1)
            ld_wu(2)
            ld_wu(3)
            ld_wd(2)
            ld_wd(3)

            psf_rr = [0]

            def next_psf():
                b = psf_rr[0] % 8
                psf_rr[0] += 1
                return b

            sqf_rr = [0]
            rl_rr = [0]

            def normF_A(h, Bhb):
                pb = next_psf()
                g = S.pe_begin([Bps[pb]])
                for kc in range(KC):
                    s = sqf_rr[0] % 4
                    sqf_rr[0] += 1
                    S.op("act", lambda e, kc=kc, s=s: e.activation(out=sqf[:, s, :], in_=h[:, kc, :], func=AF.Square),
                         reads=[Bhb], writes=[Bsqf[s]])
                    S.pe_item(g, mm(psum[pb][:, 0:N], ones1024[:], sqf[:, s, :], kc == 0, kc == KC - 1),
                              [Bsqf[s], Bconst], last=(kc == KC - 1))
                return pb

            def normF_B(pb, h, Bhb, gcol0, dst_fn, dst_bufs):
                S.op("act", lambda e: e.activation(out=scrf[:, 0, :], in_=psum[pb][:, 0:N], func=AF.Ln,
                                                   bias=dcol(CEPS), scale=1.0),
                     reads=[Bps[pb], Bconst], writes=[Bscrf[0]])
                S.op("act", lambda e: e.activation(out=scrf[:, 1, :], in_=scrf[:, 0, :], func=AF.Exp, scale=-0.5),
                     reads=[Bscrf[0]], writes=[Bscrf[1]])
                for kc in range(KC):
                    S.op("dve", lambda e, kc=kc: e.scalar_tensor_tensor(
                        out=dst_fn(kc), in0=h[:, kc, :], scalar=col(gcol0 + kc), in1=scrf[:, 1, :],
                        op0=ALU.mult, op1=ALU.mult), reads=[Bhb, Bscrf[1], Bconst], writes=[dst_bufs[kc]])

            perm = [n_tiles - 1] + list(range(n_tiles - 1))
            fbase = (n_tiles - 1) % 3

            def hb(pos):
                return (fbase + pos) % 3

            def S4(pos):
                b = hb(pos)
                pb = normF_A(hbuf[b], Bh[b])
                normF_B(pb, hbuf[b], Bh[b], G_MLP, lambda kc: u2[:, kc, :], Bu2)

            def S7_store(pos):
                b = hb(pos)
                t = perm[pos]
                h = hbuf[b]
                pb = normF_A(h, Bh[b])
                normF_B(pb, h, Bh[b], G_FIN, lambda kc: h[:, kc, :], [Bh[b]] * KC)
                if t == 0:
                    S.dma("sp", sem_hst[b], out_v[:, :, 0:N - NMETA], h[:, :, NMETA:N], reads=[Bh[b]])
                else:
                    S.dma("sp", sem_hst[b], out_v[:, :, t * N - NMETA:(t + 1) * N - NMETA], h[:], reads=[Bh[b]])

            def ld_h(pos):
                b = hb(pos)
                t = perm[pos]
                S.dma("sp", sem_hld[b], hbuf[b][:], h1_v[:, :, t * N:(t + 1) * N], writes=[Bh[b]])

            def up_group(f, fc):
                pb = next_psf()
                items = [(mm(psum[pb][:, 0:N], w_up_sb[:, kc, fc * 128:(fc + 1) * 128], u2[:, kc, :],
                             kc == 0, kc == KC - 1), [Bu2[kc], Bwu[fc // 8]]) for kc in range(KC)]
                S.pe_group(items, [Bps[pb]])
                r = rl_rr[0] % 4
                rl_rr[0] += 1
                S.op("act", lambda e: e.activation(out=rl[:, r, :], in_=psum[pb][:, 0:N], func=AF.Relu),
                     reads=[Bps[pb]], writes=[Brl[r]])
                S.op("pool" if f % 2 else "dve", lambda e: e.tensor_tensor(
                    out=act[:, f, :], in0=rl[:, r, :], in1=rl[:, r, :], op=ALU.mult),
                    reads=[Brl[r]], writes=[Bact[f]])

            def down_group(pos, half, oc):
                b = hb(pos)
                h = hbuf[b]
                pb = next_psf()
                items = [(mm(psum[pb][:, 0:N], w_dn_sb[:, half * 16 + f, oc * 128:(oc + 1) * 128],
                             act[:, f, :], f == 0, f == 15), [Bact[f], Bwd[(half * 16 + f) // 8]]) for f in range(16)]
                S.pe_group(items, [Bps[pb]])
                S.op("dve", lambda e: e.tensor_tensor(out=h[:, oc, :], in0=h[:, oc, :],
                                                      in1=psum[pb][:, 0:N], op=ALU.add),
                     reads=[Bh[b], Bps[pb]], writes=[Bh[b]])

            if n_tiles > 1:
                ld_h(1)
            S4(0)
            for pos in range(n_tiles):
                for half in range(2):
                    for f in range(16):
                        up_group(f, half * 16 + f)
                        if half == 0 and f == 3 and pos > 0:
                            S7_store(pos - 1)
                            if pos + 1 < n_tiles:
                                ld_h(pos + 1)
                    if half == 1 and pos + 1 < n_tiles:
                        S4(pos + 1)
                    for oc in range(KC):
                        down_group(pos, half, oc)
            S7_store(n_tiles - 1)
            S.barrier()
        print("sched waits:", S.nwaits, {k: v for k, v in S.latest.items()})
    return nc


def _host_prep(inp):
    f = np.float32
    x = np.asarray(inp["x"], f)
    B = x.shape[0]
    meta = np.asarray(inp["meta_tokens"], f)
    hTs = []
    for bi in range(B):
        hcat = np.concatenate([meta, x[bi]], axis=0)
        hTs.append(np.ascontiguousarray(hcat.T))
    pp = np.zeros((128, NPP), f)

    def chunked(v, n):
        return np.asarray(v, f).reshape(n, 128).T

    pp[:, G_MIX:G_MIX + 8] = chunked(inp["norm_mix_g"][0], 8)
    pp[:, G_MLP:G_MLP + 8] = chunked(inp["norm_mlp_g"][0], 8)
    pp[:, G_FIN:G_FIN + 8] = chunked(inp["norm_final_g"], 8)
    pp[:, G_OUT:G_OUT + 4] = chunked(inp["out_norm_lru_g"][0], 4)
    pp[:, G_OUT + 4:G_OUT + 8] = chunked(inp["out_norm_cc_g"][0], 4)
    c4 = np.asarray(inp["lru_conv_w"][0], f)
    pp[:, C4W:C4W + 16] = c4.reshape(4, 4, 128).transpose(2, 1, 0).reshape(128, 16)
    pp[:, C4B:C4B + 4] = chunked(inp["lru_conv_b"][0], 4)
    pp[:, GAB:GAB + 4] = chunked(inp["lru_gate_a_b"][0], 4)
    pp[:, GXB:GXB + 4] = chunked(inp["lru_gate_x_b"][0], 4)
    pp[:, APAR:APAR + 4] = chunked(inp["lru_a_param"][0], 4)
    c31 = np.asarray(inp["cc_conv_w"][0], f)
    pp[:, C31W:C31W + 124] = c31.reshape(31, 4, 128).transpose(2, 1, 0).reshape(128, 124)
    pp[:, C31B:C31B + 4] = chunked(inp["cc_conv_b"][0], 4)
    pp[:, LNG:LNG + 4] = chunked(inp["cc_ln_g"][0], 4)
    pp[:, LNB:LNB + 4] = chunked(inp["cc_ln_b"][0], 4)
    pp[:, IDENT:IDENT + 64] = np.concatenate([np.eye(64, dtype=f), np.eye(64, dtype=f)], axis=0)
    gwa = np.asarray(inp["lru_gate_a_w"][0], f)
    gwx = np.asarray(inp["lru_gate_x_w"][0], f)
    gw = np.zeros((128, 8, 128), f)
    for j in range(4):
        for hh in range(2):
            gw[hh * 64:(hh + 1) * 64, j, hh * 64:(hh + 1) * 64] = gwa[2 * j + hh]
            gw[hh * 64:(hh + 1) * 64, 4 + j, hh * 64:(hh + 1) * 64] = gwx[2 * j + hh]
    shared = {
        "w_in": np.ascontiguousarray(inp["w_in"][0], f),
        "w_out": np.ascontiguousarray(inp["w_out"][0], f),
        "w_up": np.ascontiguousarray(inp["w_mlp_up"][0], f),
        "w_down": np.ascontiguousarray(inp["w_mlp_down"][0], f),
        "gw": gw, "pp": pp,
    }
    return hTs, shared


def kernel(**inputs):
    hTs, shared = _host_prep(inputs)
    B = len(hTs)
    T = hTs[0].shape[1]
    assert T % N == 0
    n_tiles = T // N
    nc = bass.Bass("TRN2", target_bir_lowering=False)
    _build(nc, n_tiles)
    in_maps = [dict(shared, hT=hTs[bi]) for bi in range(B)]
    res = run_bass_kernel_spmd(nc, in_maps, core_ids=list(range(B)))
    if DEBUG:
        global LAST_DBG
        LAST_DBG = np.asarray(res.results[0]["dbg"])
    out = np.stack([np.ascontiguousarray(np.asarray(r["outT"]).T) for r in res.results], axis=0)
    return out.astype(np.float32)
```

```python
import math
import numpy as np
import concourse.bass as bass
import concourse.mybir as mybir
from concourse.bass_utils import run_bass_kernel_spmd

F32 = mybir.dt.float32
BF16 = mybir.dt.bfloat16
AF = mybir.ActivationFunctionType
ALU = mybir.AluOpType

D = 1024
KC = 8
NMETA = 16
DFF = 4096
FC = 32
EPS = 1e-6
N = 456
DEBUG = []
GELU_K = math.sqrt(2.0 / math.pi)

G_MIX, G_MLP, G_FIN, G_OUT = 0, 8, 16, 24
C4W, C4B, GAB, GXB, APAR = 32, 48, 52, 56, 60
C31W, C31B, LNG, LNB, IDENT = 64, 188, 192, 196, 200
NPP = 264
HC, HGAB, HGXB, HLNG, HLNB, CEPS, CEPS4, CLN05, CONE, DV_N = 0, 4, 8, 12, 16, 20, 21, 22, 23, 24


class _Eng:
    def __init__(self, name, eng, sem, kind):
        self.name, self.eng, self.sem, self.kind = name, eng, sem, kind
        self.count = 0
        self.known = {}
        self.snaps = {}


class _Buf:
    __slots__ = ("name", "w", "r")

    def __init__(self, name):
        self.name = name
        self.w = None
        self.r = []


class Sched:
    def __init__(self, nc, stack):
        self.nc = nc
        self.stack = stack
        self.sems = {}
        self.latest = {}
        self.E = {}
        for name, eng, kind in (("pe", nc.tensor, "pe"), ("act", nc.scalar, "c"),
                                ("dve", nc.vector, "c"), ("pool", nc.gpsimd, "c"),
                                ("sp", nc.sync, "q")):
            sem = stack.enter_context(nc.semaphore("s_" + name))
            self.sems[name] = sem
            self.latest[name] = 0
            self.E[name] = _Eng(name, eng, sem, kind)
        self.nwaits = 0

    def dma_sem(self, name):
        sem = self.stack.enter_context(self.nc.semaphore("d_" + name))
        self.sems[name] = sem
        self.latest[name] = 0
        return name

    def _need(self, e, reads, writes, need):
        for b in reads:
            if b.w is not None:
                self._add(e, b.w, need, False)
        for b in writes:
            if b.w is not None:
                self._add(e, b.w, need, False)
            for ev in b.r:
                self._add(e, ev, need, True)

    def _add(self, e, ev, need, war):
        key, val, src = ev
        if src == e.name and e.kind == "pe":
            return
        if val > need.get(key, 0):
            need[key] = val

    def _emit_waits(self, e, need):
        for key, val in need.items():
            if val > e.known.get(key, 0):
                e.eng.wait_ge(self.sems[key], val)
                self.nwaits += 1
                e.known[key] = val
                src = self.E.get(key)
                if src is not None and src is not e:
                    snap = src.snaps.get(val)
                    if snap:
                        for k2, v2 in snap.items():
                            if v2 > e.known.get(k2, 0):
                                e.known[k2] = v2

    def _commit(self, ev, reads, writes):
        for b in writes:
            b.w = ev
            b.r = []
        for b in reads:
            if b in writes:
                continue
            b.r = [x for x in b.r if not (x[0] == ev[0])] + [ev]

    def op(self, engname, fn, reads=(), writes=()):
        e = self.E[engname]
        need = {}
        self._need(e, reads, writes, need)
        self._emit_waits(e, need)
        ins = fn(e.eng)
        ins.then_inc(e.sem, 1)
        e.count += 1
        self.latest[engname] = e.count
        e.snaps[e.count] = dict(e.known)
        self._commit((engname, e.count, engname), reads, writes)

    def pe_group(self, items, writes):
        e = self.E["pe"]
        need = {}
        self._need(e, (), writes, need)
        self._emit_waits(e, need)
        allreads = []
        ins = None
        for fn, reads in items:
            need = {}
            self._need(e, reads, (), need)
            self._emit_waits(e, need)
            ins = fn(e.eng)
            for b in reads:
                if b not in allreads:
                    allreads.append(b)
        ins.then_inc(e.sem, 1)
        e.count += 1
        self.latest["pe"] = e.count
        e.snaps[e.count] = dict(e.known)
        self._commit(("pe", e.count, "pe"), allreads, writes)

    def pe_begin(self, writes):
        e = self.E["pe"]
        need = {}
        self._need(e, (), writes, need)
        self._emit_waits(e, need)
        return {"writes": writes, "reads": []}

    def pe_item(self, g, fn, reads, last=False):
        e = self.E["pe"]
        need = {}
        self._need(e, reads, (), need)
        self._emit_waits(e, need)
        ins = fn(e.eng)
        for b in reads:
            if b not in g["reads"]:
                g["reads"].append(b)
        ins.then_inc(e.sem, 1)
        e.count += 1
        self.latest["pe"] = e.count
        e.snaps[e.count] = dict(e.known)
        ev = ("pe", e.count, "pe")
        if last:
            self._commit(ev, g["reads"], g["writes"])
        else:
            for b in reads:
                b.r = [x for x in b.r if not (x[0] == "pe")] + [ev]

    def dma(self, qname, semname, out, in_, reads=(), writes=(), **kw):
        e = self.E[qname]
        need = {}
        for b in reads:
            if b.w is not None and b.w[1] > need.get(b.w[0], 0):
                need[b.w[0]] = b.w[1]
        for b in writes:
            if b.w is not None and b.w[1] > need.get(b.w[0], 0):
                need[b.w[0]] = b.w[1]
            for ev in b.r:
                if ev[1] > need.get(ev[0], 0):
                    need[ev[0]] = ev[1]
        self._emit_waits(e, need)
        e.eng.dma_start(out=out, in_=in_, **kw).then_inc(self.sems[semname], 16)
        self.latest[semname] += 16
        self._commit((semname, self.latest[semname], "dma:" + qname), reads, writes)

    def barrier(self):
        for e in self.E.values():
            need = {k: v for k, v in self.latest.items() if v > 0 and k != e.name}
            self._emit_waits(e, need)


def _build(nc, n_tiles):
    from contextlib import ExitStack
    T = n_tiles * N
    TOUT = T - NMETA

    hT = nc.dram_tensor("hT", [D, T], F32, kind="ExternalInput").ap()
    w_in = nc.dram_tensor("w_in", [D, 2 * D], F32, kind="ExternalInput").ap()
    w_out = nc.dram_tensor("w_out", [D, D], F32, kind="ExternalInput").ap()
    w_up = nc.dram_tensor("w_up", [D, DFF], F32, kind="ExternalInput").ap()
    w_down = nc.dram_tensor("w_down", [DFF, D], F32, kind="ExternalInput").ap()
    gw = nc.dram_tensor("gw", [128, 8, 128], F32, kind="ExternalInput").ap()
    pp = nc.dram_tensor("pp", [128, NPP], F32, kind="ExternalInput").ap()
    outT = nc.dram_tensor("outT", [D, TOUT], F32, kind="ExternalOutput").ap()
    dbg = nc.dram_tensor("dbg", [len(DEBUG), 128, N], F32, kind="ExternalOutput").ap() if DEBUG else None
    h1s = nc.dram_tensor("h1s", [D, T], F32, kind="Internal").ap()
    wub = nc.dram_tensor("wub", [D, DFF], BF16, kind="Internal").ap()
    wdb = nc.dram_tensor("wdb", [DFF, D], BF16, kind="Internal").ap()

    hT_v = hT.rearrange("(kc p) n -> p kc n", p=128)
    h1_v = h1s.rearrange("(kc p) n -> p kc n", p=128)
    out_v = outT.rearrange("(kc p) n -> p kc n", p=128)

    with ExitStack() as top:
        S = Sched(nc, top)

        def sb(name, shape, dt, stack=top):
            return stack.enter_context(nc.sbuf_tensor(name, shape, dt))

        hbuf = [sb("h%d" % i, [128, KC, N], F32) for i in range(3)]
        Bh = [_Buf("h%d" % i) for i in range(3)]
        pp_sb = sb("pp_sb", [128, NPP], F32)
        dv = sb("dv", [128, DV_N], F32)
        ones1024 = sb("ones1024", [128, 128], BF16)
        psum = [top.enter_context(nc.psum_tensor("ps%d" % i, [128, 512], F32)) for i in range(8)]
        Bps = [_Buf("ps%d" % i) for i in range(8)]
        Bconst = _Buf("const")
        sem_hld = [S.dma_sem("hld%d" % i) for i in range(3)]
        sem_hst = [S.dma_sem("hst%d" % i) for i in range(3)]
        sem_w = S.dma_sem("wld")
        sem_c = S.dma_sem("cld")

        sem_dbg = S.dma_sem("dbg") if DEBUG else None

        def dump(name, ap, bufs, t):
            if DEBUG and name in DEBUG and t == 0:
                S.dma("pool", sem_dbg, dbg[DEBUG.index(name)], ap, reads=bufs)

        def col(c, n=1):
            return pp_sb[:, c:c + n]

        def dcol(c, n=1):
            return dv[:, c:c + n]

        with ExitStack() as pm:
            w_in_sb = sb("w_in_sb", [128, KC, 2 * D], BF16, pm)
            w_out_sb = sb("w_out_sb", [128, KC, D], BF16, pm)
            gw_sb = sb("gw_sb", [128, 8, 128], BF16, pm)
            d4 = sb("d4", [128, 16, 128], BF16, pm)
            d31 = sb("d31", [128, 124, 64], BF16, pm)
            ones512 = sb("ones512", [128, 128], BF16, pm)
            u = sb("u", [128, KC, N], BF16, pm)
            sq = sb("sq", [128, 8, N], BF16, pm)
            yn = sb("yn", [128, KC, N], BF16, pm)
            scr = sb("scr", [128, 5, N], F32, pm)
            xl_bf = sb("xl_bf", [128, 4, N + 3], BF16, pm)
            xc32 = sb("xc32", [128, 4, N], F32, pm)
            xcb = sb("xcb", [128, 2, N], BF16, pm)
            rbuf = sb("rbuf", [128, 4, N], F32, pm)
            ibuf = sb("ibuf", [128, 4, N], F32, pm)
            abuf = sb("abuf", [128, 4, N], F32, pm)
            hs = sb("hs", [128, 4, N], F32, pm)
            gl = sb("gl", [128, 4, N], F32, pm)
            tg = sb("tg", [128, 2, N], F32, pm)
            c_bf = sb("c_bf", [128, 4, N + 30], BF16, pm)
            cc32 = sb("cc32", [128, 4, N], F32, pm)
            carry = sb("carry", [128, 4], F32, pm)
            print("SBUF remaining after phase M alloc:", nc.sbuf_bytes_remaining)

            Bu = [_Buf("u%d" % i) for i in range(KC)]
            Bsq = [_Buf("sq%d" % i) for i in range(8)]
            Bscr = [_Buf("scr%d" % i) for i in range(5)]
            Bxl = [_Buf("xl%d" % i) for i in range(4)]
            Bxlh = [_Buf("xlh%d" % i) for i in range(4)]
            Bxc = [_Buf("xc%d" % i) for i in range(4)]
            Bxcb = [_Buf("xcb%d" % i) for i in range(4)]
            Br = [_Buf("r%d" % i) for i in range(4)]
            Bi = [_Buf("i%d" % i) for i in range(4)]
            Ba = [_Buf("a%d" % i) for i in range(4)]
            Bhs = [_Buf("hs%d" % i) for i in range(4)]
            Bgl = [_Buf("gl%d" % i) for i in range(4)]
            Byn = [_Buf("yn%d" % i) for i in range(KC)]
            Btg = [_Buf("tg%d" % i) for i in range(2)]
            Bc = [_Buf("c%d" % i) for i in range(4)]
            Bch = [_Buf("ch%d" % i) for i in range(4)]
            Bcc = [_Buf("cc%d" % i) for i in range(4)]
            tz = xc32
            Btz = [Bxc[2], Bxc[3]]
            Bcarry = [_Buf("carry%d" % i) for i in range(4)]

            S.dma("sp", sem_c, pp_sb[:], pp, writes=[Bconst])
            sem_gw = S.dma_sem("gwld")
            sem_wout = S.dma_sem("woutld")
            Bgw, Bw_out, Bd4, Bd31 = (_Buf("gw"), _Buf("w_out"), _Buf("d4"), _Buf("d31"))
            w_in_v = w_in.rearrange("(kc p) o -> p kc o", p=128)
            w_out_v = w_out.rearrange("(kc p) o -> p kc o", p=128)
            Bw_in = [_Buf("w_in%d" % i) for i in range(4)]
            sem_wi = [S.dma_sem("wi%d" % i) for i in range(4)]
            for q in (0, 3, 2, 1):
                S.dma("pool", sem_wi[q], w_in_sb[:, :, q * 512:(q + 1) * 512], w_in_v[:, :, q * 512:(q + 1) * 512],
                      writes=[Bw_in[q]])
            S.dma("pool", sem_gw, gw_sb[:], gw, writes=[Bgw])
            for q in range(2):
                S.dma("pool", sem_wout, w_out_sb[:, 4 * q:4 * q + 4, :], w_out_v[:, 4 * q:4 * q + 4, :],
                      writes=[_Buf("wp")])
            Bw_out.w = (sem_wout, S.latest[sem_wout], "dma:pool")
            S.dma("sp", sem_hld[0], hbuf[0][:], hT_v[:, :, 0:N], writes=[Bh[0]])

            S.op("dve", lambda e: e.memset(ones1024[:], 1.0 / 1024.0), writes=[Bconst])
            S.op("dve", lambda e: e.memset(ones512[:], 1.0 / 512.0), writes=[Bconst])
            S.op("dve", lambda e: e.memset(dcol(CEPS), EPS), writes=[Bconst])
            S.op("dve", lambda e: e.memset(dcol(CEPS4), 4.0 * EPS), writes=[Bconst])
            S.op("dve", lambda e: e.memset(dcol(CLN05), math.log(0.5)), writes=[Bconst])
            S.op("dve", lambda e: e.memset(dcol(CONE), 1.0), writes=[Bconst])
            S.op("dve", lambda e: e.memset(carry[:], 0.0), writes=Bcarry)
            for j in range(4):
                S.op("dve", lambda e, j=j: e.memset(xl_bf[:, j, 0:3], 0.0), writes=[Bxlh[j]])
                S.op("dve", lambda e, j=j: e.memset(c_bf[:, j, 0:30], 0.0), writes=[Bch[j]])
            S.op("act", lambda e: e.activation(out=dcol(HC, 4), in_=col(APAR, 4), func=AF.Exp, scale=-1.0),
                 reads=[Bconst], writes=[Bconst])
            ecol = dcol(HC, 4)
            tcol = dcol(HGAB, 4)
            S.op("dve", lambda e: e.tensor_scalar(out=tcol, in0=ecol, scalar1=-1.0 / 5.0, scalar2=1.0 / 4.0,
                                                  op0=ALU.mult, op1=ALU.add), reads=[Bconst], writes=[Bconst])
            for cst in (-1.0 / 3.0, 1.0 / 2.0, -1.0):
                S.op("dve", lambda e: e.tensor_tensor(out=tcol, in0=tcol, in1=ecol, op=ALU.mult),
                     reads=[Bconst], writes=[Bconst])
                S.op("dve", lambda e, cst=cst: e.tensor_scalar(out=tcol, in0=tcol, scalar1=-1.0, scalar2=abs(cst),
                                                               op0=ALU.mult, op1=ALU.add),
                     reads=[Bconst], writes=[Bconst])
            S.op("dve", lambda e: e.tensor_tensor(out=ecol, in0=tcol, in1=ecol, op=ALU.mult),
                 reads=[Bconst], writes=[Bconst])
            S.op("dve", lambda e: e.tensor_scalar(out=ecol, in0=ecol, scalar1=-4.0, scalar2=None, op0=ALU.mult),
                 reads=[Bconst], writes=[Bconst])
            for dst, src in ((HGAB, GAB), (HGXB, GXB), (HLNG, LNG), (HLNB, LNB)):
                S.op("dve", lambda e, dst=dst, src=src: e.tensor_scalar(
                    out=dcol(dst, 4), in0=col(src, 4), scalar1=0.5, scalar2=None, op0=ALU.mult),
                    reads=[Bconst], writes=[Bconst])
            pph = pp_sb[:].tensor

            def bc(off, steps):
                return bass.AP(pph, off, [[NPP, 128]] + steps)

            S.op("dve", lambda e: e.memset(d4[:], 0.0), writes=[Bd4])
            for hb in (0, 64):
                S.op("dve", lambda e, hb=hb: e.tensor_tensor(
                    out=d4[hb:hb + 64, :, hb:hb + 64],
                    in0=bass.AP(pph, hb * NPP + IDENT, [[NPP, 64], [0, 16], [1, 64]]),
                    in1=bass.AP(pph, hb * NPP + C4W, [[NPP, 64], [1, 16], [0, 64]]), op=ALU.mult),
                    reads=[Bconst], writes=[Bd4])
            for j in range(4):
                S.op("dve", lambda e, j=j: e.scalar_tensor_tensor(
                    out=d31[:, j * 31:(j + 1) * 31, :], in0=bc(IDENT, [[0, 31], [1, 64]]), scalar=0.5,
                    in1=bc(C31W + 31 * j, [[1, 31], [0, 64]]), op0=ALU.mult, op1=ALU.mult),
                    reads=[Bconst], writes=[Bd31])

            ps_rr = [0]

            def next_ps():
                b = ps_rr[0] % 7
                ps_rr[0] += 1
                return b

            ln_banks = {}

            def mm(out_ap, lhsT, rhs, start, stop):
                return lambda e: e.matmul(out_ap, lhsT, rhs, start=start, stop=stop)

            def mm2(out_ap, lhsT, rhs, start, stop, base):
                return lambda e: e.matmul(out_ap, lhsT, rhs, start=start, stop=stop, tile_position=(base, base))

            def rstd_from(ps_b, epscol, si):
                S.op("act", lambda e: e.activation(out=scr[:, si, :], in_=psum[ps_b][:, 0:N], func=AF.Ln,
                                                   bias=dcol(epscol), scale=1.0),
                     reads=[Bps[ps_b], Bconst], writes=[Bscr[si]])
                S.op("act", lambda e: e.activation(out=scr[:, si, :], in_=scr[:, si, :], func=AF.Exp,
                                                   scale=-0.5),
                     reads=[Bscr[si]], writes=[Bscr[si]])

            sq_rr = [0]

            def next_sq():
                b = sq_rr[0] % 8
                sq_rr[0] += 1
                return b

            def norm_sq(src_aps, src_bufs, scratch=None):
                out = []
                for i in range(len(src_aps)):
                    if scratch is None:
                        s = next_sq()
                        dst, dbuf = sq[:, s, :], Bsq[s]
                    else:
                        dst, dbuf = scratch[i]
                    S.op("act", lambda e, i=i, dst=dst: e.activation(out=dst, in_=src_aps[i], func=AF.Square),
                         reads=[src_bufs[i]], writes=[dbuf])
                    out.append((dst, dbuf))
                return out

            def norm_mm(sqs, ones_t, ps_b):
                n = len(sqs)
                S.pe_group([(mm(psum[ps_b][:, 0:N], ones_t[:], sqs[i][0], i == 0, i == n - 1), [sqs[i][1], Bconst])
                            for i in range(n)], [Bps[ps_b]])

            def win_group(oc):
                pb = next_ps()
                items = [(mm(psum[pb][:, 0:N], w_in_sb[:, kc, oc * 128:(oc + 1) * 128], u[:, kc, :],
                             kc == 0, kc == KC - 1), [Bu[kc], Bw_in[oc // 4]]) for kc in range(KC)]
                S.pe_group(items, [Bps[pb]])
                return pb

            p1_state = {}

            def P1_sq(t):
                b = t % 3
                h = hbuf[b]
                p1_state[t] = norm_sq([h[:, kc, :] for kc in range(KC)], [Bh[b]] * KC,
                                      scratch=[(u[:, kc, :], Bu[kc]) for kc in range(KC)])

            def P1(t):
                b = t % 3
                h = hbuf[b]
                if t not in p1_state:
                    P1_sq(t)
                psb = next_ps()
                norm_mm(p1_state.pop(t), ones1024, psb)
                rstd_from(psb, CEPS, 0)
                for kc in range(KC):
                    S.op("dve", lambda e, kc=kc: e.scalar_tensor_tensor(
                        out=u[:, kc, :], in0=h[:, kc, :], scalar=col(G_MIX + kc), in1=scr[:, 0, :],
                        op0=ALU.mult, op1=ALU.mult), reads=[Bh[b], Bscr[0], Bconst], writes=[Bu[kc]])

            def W_xl(j):
                pb = win_group(j)
                S.op("act", lambda e: e.activation(out=xl_bf[:, j, 3:3 + N], in_=psum[pb][:, 0:N], func=AF.Copy),
                     reads=[Bps[pb]], writes=[Bxl[j]])

            def W_cc(j):
                pgc = win_group(12 + j)
                tgi = j % 2
                S.op("act", lambda e: e.activation(out=tg[:, tgi, :], in_=psum[pgc][:, 0:N], func=AF.Tanh, scale=0.5),
                     reads=[Bps[pgc]], writes=[Btg[tgi]])
                pv = win_group(8 + j)
                S.op("dve", lambda e: e.scalar_tensor_tensor(
                    out=c_bf[:, j, 30:30 + N], in0=tg[:, tgi, :], scalar=1.0, in1=psum[pv][:, 0:N],
                    op0=ALU.add, op1=ALU.mult), reads=[Btg[tgi], Bps[pv]], writes=[Bc[j]])

            def W_gl(j):
                pgl = win_group(4 + j)
                S.op("act", lambda e: e.activation(out=gl[:, j, :], in_=psum[pgl][:, 0:N], func=AF.Gelu_apprx_tanh),
                     reads=[Bps[pgl]], writes=[Bgl[j]])

            def conv4_unit(j):
                pc = next_ps()
                items = [(mm(psum[pc][:, 0:N], d4[:, j * 4 + k, :], xl_bf[:, j, k:k + N], k == 0, k == 3),
                          [Bxl[j], Bxlh[j], Bd4]) for k in range(4)]
                S.pe_group(items, [Bps[pc]])
                S.op("act", lambda e: e.activation(out=xcb[:, j % 2, :], in_=psum[pc][:, 0:N],
                                                   func=AF.Identity, bias=col(C4B + j), scale=1.0),
                     reads=[Bps[pc], Bconst], writes=[Bxcb[j % 2]])
                S.op("act", lambda e: e.activation(out=xc32[:, j, :], in_=psum[pc][:, 0:N],
                                                   func=AF.Identity, bias=col(C4B + j), scale=1.0),
                     reads=[Bps[pc], Bconst], writes=[Bxc[j]])
                S.op("pool", lambda e: e.tensor_copy(out=xl_bf[:, j, 0:3], in_=xl_bf[:, j, N:N + 3]),
                     reads=[Bxl[j]], writes=[Bxlh[j]])

            def gates_unit(j):
                pga = next_ps()
                S.pe_group([(mm(psum[pga][:, 0:N], gw_sb[:, j, :], xcb[:, j % 2, :], True, True),
                             [Bxcb[j % 2], Bgw])], [Bps[pga]])
                pgx = next_ps()
                S.pe_group([(mm(psum[pgx][:, 0:N], gw_sb[:, 4 + j, :], xcb[:, j % 2, :], True, True),
                             [Bxcb[j % 2], Bgw])], [Bps[pgx]])
                S.op("act", lambda e: e.activation(out=rbuf[:, j, :], in_=psum[pga][:, 0:N],
                                                   func=AF.Tanh, bias=dcol(HGAB + j), scale=0.5),
                     reads=[Bps[pga], Bconst], writes=[Br[j]])
                S.op("act", lambda e: e.activation(out=ibuf[:, j, :], in_=psum[pgx][:, 0:N],
                                                   func=AF.Tanh, bias=dcol(HGXB + j), scale=0.5),
                     reads=[Bps[pgx], Bconst], writes=[Bi[j]])
                S.op("act", lambda e: e.activation(out=abuf[:, j, :], in_=rbuf[:, j, :], func=AF.Exp,
                                                   bias=dcol(HC + j), scale=dcol(HC + j)),
                     reads=[Br[j], Bconst], writes=[Ba[j]])
                S.op("pool", lambda e: e.tensor_tensor(out=rbuf[:, j, :], in0=abuf[:, j, :],
                                                       in1=abuf[:, j, :], op=ALU.mult),
                     reads=[Ba[j]], writes=[Br[j]])
                S.op("dve", lambda e: e.scalar_tensor_tensor(
                    out=ibuf[:, j, :], in0=ibuf[:, j, :], scalar=1.0, in1=xc32[:, j, :],
                    op0=ALU.add, op1=ALU.mult), reads=[Bi[j], Bxc[j]], writes=[Bi[j]])

            def conv31_unit(j):
                pa, pb2 = next_ps(), next_ps()
                items = []
                for k in range(31):
                    items.append((mm2(psum[pa][0:64, 0:N], d31[0:64, j * 31 + k, :], c_bf[0:64, j, k:k + N],
                                      k == 0, k == 30, 0), [Bc[j], Bch[j], Bd31]))
                    items.append((mm2(psum[pb2][64:128, 0:N], d31[64:128, j * 31 + k, :], c_bf[64:128, j, k:k + N],
                                      k == 0, k == 30, 64), [Bc[j], Bch[j], Bd31]))
                S.pe_group(items, [Bps[pa], Bps[pb2]])
                S.op("act", lambda e: e.activation(out=cc32[0:64, j, :], in_=psum[pa][0:64, 0:N],
                                                   func=AF.Identity, bias=pp_sb[0:64, C31B + j:C31B + j + 1], scale=1.0),
                     reads=[Bps[pa], Bconst], writes=[Bcc[j]])
                S.op("act", lambda e: e.activation(out=cc32[64:128, j, :], in_=psum[pb2][64:128, 0:N],
                                                   func=AF.Identity, bias=pp_sb[64:128, C31B + j:C31B + j + 1], scale=1.0),
                     reads=[Bps[pb2], Bconst], writes=[Bcc[j]])
                S.op("pool", lambda e: e.tensor_copy(out=c_bf[:, j, 0:30], in_=c_bf[:, j, N:N + 30]),
                     reads=[Bc[j]], writes=[Bch[j]])
                s1, s2 = 2 * j, 2 * j + 1
                S.op("dve", lambda e: e.tensor_copy(out=sq[:, s1, :], in_=cc32[:, j, :]),
                     reads=[Bcc[j]], writes=[Bsq[s1]])
                S.op("dve", lambda e: e.tensor_tensor(out=sq[:, s2, :], in0=cc32[:, j, :],
                                                      in1=cc32[:, j, :], op=ALU.mult),
                     reads=[Bcc[j]], writes=[Bsq[s2]])
                return None

            def conv31_stats(js):
                return None

            def ln_stat_groups():
                pm_, pq_ = 7, next_ps()
                ln_banks["mean"], ln_banks["msq"] = pm_, pq_
                S.pe_group([(mm(psum[pm_][:, 0:N], ones512[:], sq[:, 2 * j, :], j == 0, j == 3),
                             [Bsq[2 * j], Bconst]) for j in range(4)], [Bps[pm_]])
                S.pe_group([(mm(psum[pq_][:, 0:N], ones512[:], sq[:, 2 * j + 1, :], j == 0, j == 3),
                             [Bsq[2 * j + 1], Bconst]) for j in range(4)], [Bps[pq_]])
                sq_rr[0] = 0

            def lru_tail_act():
                for j in range(4):
                    S.op("act", lambda e, j=j: e.activation(out=rbuf[:, j, :], in_=rbuf[:, j, :], func=AF.Ln,
                                                            bias=dcol(CONE), scale=-1.0),
                         reads=[Br[j], Bconst], writes=[Br[j]])
                for j in range(4):
                    S.op("act", lambda e, j=j: e.activation(out=rbuf[:, j, :], in_=rbuf[:, j, :], func=AF.Exp,
                                                            bias=dcol(CLN05), scale=0.5),
                         reads=[Br[j], Bconst], writes=[Br[j]])

            def lru_tail_dve(j):
                S.op("dve", lambda e: e.tensor_tensor(out=ibuf[:, j, :], in0=ibuf[:, j, :],
                                                      in1=rbuf[:, j, :], op=ALU.mult),
                     reads=[Bi[j], Br[j]], writes=[Bi[j]])
                S.op("dve", lambda e: e.tensor_tensor_scan(
                    out=hs[:, j, :], data0=abuf[:, j, :], data1=ibuf[:, j, :], initial=carry[:, j:j + 1],
                    op0=ALU.mult, op1=ALU.add), reads=[Ba[j], Bi[j], Bcarry[j]], writes=[Bhs[j]])
                S.op("dve", lambda e: e.tensor_copy(out=carry[:, j:j + 1], in_=hs[:, j, N - 1:N]),
                     reads=[Bhs[j]], writes=[Bcarry[j]])
                S.op("dve", lambda e: e.tensor_tensor(out=hs[:, j, :], in0=hs[:, j, :],
                                                      in1=gl[:, j, :], op=ALU.mult),
                     reads=[Bhs[j], Bgl[j]], writes=[Bhs[j]])

            st_state = {}

            def s_lru_sq():
                st_state["lru"] = norm_sq([hs[:, j, :] for j in range(4)], Bhs)

            def s_lru_stats():
                p_lru = next_ps()
                norm_mm(st_state.pop("lru"), ones512, p_lru)
                rstd_from(p_lru, CEPS, 3)

            def s_cc_sq():
                st_state["cc"] = norm_sq([cc32[:, j, :] for j in range(4)], Bcc)

            def s_ln_stats():
                pm_, pq_ = ln_banks["mean"], ln_banks["msq"]
                S.op("act", lambda e: e.activation(out=scr[:, 1, :], in_=psum[pm_][:, 0:N], func=AF.Square),
                     reads=[Bps[pm_]], writes=[Bscr[1]])
                S.op("dve", lambda e: e.tensor_tensor(out=scr[:, 1, :], in0=psum[pq_][:, 0:N], in1=scr[:, 1, :],
                                                      op=ALU.subtract), reads=[Bps[pq_], Bscr[1]], writes=[Bscr[1]])
                S.op("act", lambda e: e.activation(out=scr[:, 1, :], in_=scr[:, 1, :], func=AF.Ln,
                                                   bias=dcol(CEPS), scale=1.0),
                     reads=[Bscr[1], Bconst], writes=[Bscr[1]])
                S.op("act", lambda e: e.activation(out=scr[:, 1, :], in_=scr[:, 1, :], func=AF.Exp, scale=-0.5),
                     reads=[Bscr[1]], writes=[Bscr[1]])
                S.op("dve", lambda e: e.tensor_tensor(out=scr[:, 2, :], in0=psum[pm_][:, 0:N], in1=scr[:, 1, :],
                                                      op=ALU.mult), reads=[Bps[pm_], Bscr[1]], writes=[Bscr[2]])

            def s_z(j):
                S.op("dve", lambda e: e.tensor_tensor(out=cc32[:, j, :], in0=cc32[:, j, :], in1=scr[:, 1, :],
                                                      op=ALU.mult), reads=[Bcc[j], Bscr[1]], writes=[Bcc[j]])
                S.op("dve", lambda e: e.tensor_tensor(out=cc32[:, j, :], in0=cc32[:, j, :], in1=scr[:, 2, :],
                                                      op=ALU.subtract), reads=[Bcc[j], Bscr[2]], writes=[Bcc[j]])

            def s_silu(j):
                zi = j % 2
                zs = 2 + zi
                S.op("act", lambda e: e.activation(out=tz[:, zs, :], in_=cc32[:, j, :], func=AF.Tanh,
                                                   bias=dcol(HLNB + j), scale=dcol(HLNG + j)),
                     reads=[Bcc[j], Bconst], writes=[Btz[zi]])
                S.op("pool", lambda e: e.tensor_scalar(out=cc32[:, j, :], in0=cc32[:, j, :],
                                                       scalar1=col(LNG + j), scalar2=col(LNB + j),
                                                       op0=ALU.mult, op1=ALU.add),
                     reads=[Bcc[j], Bconst], writes=[Bcc[j]])
                S.op("dve", lambda e: e.scalar_tensor_tensor(
                    out=cc32[:, j, :], in0=tz[:, zs, :], scalar=1.0, in1=cc32[:, j, :],
                    op0=ALU.add, op1=ALU.mult), reads=[Btz[zi], Bcc[j]], writes=[Bcc[j]])

            def s_cc_stats():
                p_cc = next_ps()
                norm_mm(st_state.pop("cc"), ones512, p_cc)
                rstd_from(p_cc, CEPS4, 4)

            def s_yn_lru():
                for j in range(4):
                    S.op("dve", lambda e, j=j: e.scalar_tensor_tensor(
                        out=yn[:, j, :], in0=hs[:, j, :], scalar=col(G_OUT + j), in1=scr[:, 3, :],
                        op0=ALU.mult, op1=ALU.mult), reads=[Bhs[j], Bscr[3], Bconst], writes=[Byn[j]])

            def s_yn_cc():
                for j in range(4):
                    S.op("dve", lambda e, j=j: e.scalar_tensor_tensor(
                        out=yn[:, 4 + j, :], in0=cc32[:, j, :], scalar=col(G_OUT + 4 + j), in1=scr[:, 4, :],
                        op0=ALU.mult, op1=ALU.mult), reads=[Bcc[j], Bscr[4], Bconst], writes=[Byn[4 + j]])

            def wout_unit(t, oc):
                b = t % 3
                h = hbuf[b]
                pb = next_ps()
                items = [(mm(psum[pb][:, 0:N], w_out_sb[:, kc, oc * 128:(oc + 1) * 128], yn[:, kc, :],
                             kc == 0, kc == KC - 1), [Byn[kc], Bw_out]) for kc in range(KC)]
                S.pe_group(items, [Bps[pb]])
                S.op("dve", lambda e: e.tensor_tensor(out=h[:, oc, :], in0=h[:, oc, :],
                                                      in1=psum[pb][:, 0:N], op=ALU.add),
                     reads=[Bh[b], Bps[pb]], writes=[Bh[b]])

            def finish_tile(t):
                b = t % 3
                if t == n_tiles - 1:
                    return
                S.dma("sp", sem_hst[b], h1_v[:, :, t * N:(t + 1) * N], hbuf[b][:], reads=[Bh[b]])

            if n_tiles > 1:
                S.dma("sp", sem_hld[1], hbuf[1][:], hT_v[:, :, N:2 * N], writes=[Bh[1]])
            P1(0)
            for j in range(4):
                W_xl(j)
            for j in range(4):
                W_cc(j)
            for j in range(4):
                W_gl(j)
            sem_cv = S.dma_sem("wconv")
            Bwub, Bwdb = _Buf("wub"), _Buf("wdb")
            conv_jobs = []
            for q in range(8):
                conv_jobs.append((wub[q * 128:(q + 1) * 128, :], w_up[q * 128:(q + 1) * 128, :]))
            for q in range(8):
                conv_jobs.append((wdb[q * 512:(q + 1) * 512, :], w_down[q * 512:(q + 1) * 512, :]))
            n_conv_tiles = max(1, min(4, n_tiles - 1))

            def emit_conv(tt):
                per = (len(conv_jobs) + n_conv_tiles - 1) // n_conv_tiles
                for o_ap, i_ap in conv_jobs[tt * per:(tt + 1) * per]:
                    S.dma("pool", sem_cv, o_ap, i_ap, writes=[_Buf("wp")])

            for t in range(n_tiles):
                nxt = t + 1 < n_tiles
                if t < n_conv_tiles:
                    emit_conv(t)
                if t >= 1 and nxt:
                    bb = (t + 1) % 3
                    S.dma("sp", sem_hld[bb], hbuf[bb][:], hT_v[:, :, (t + 1) * N:(t + 2) * N], writes=[Bh[bb]])
                fill = [(lambda oc=oc: wout_unit(t - 1, oc)) for oc in range(KC)] if t > 0 else []
                conv4_unit(0)
                conv4_unit(1)
                gates_unit(0)
                for j in range(2, 4):
                    if fill:
                        fill.pop(0)()
                    conv4_unit(j)
                    if fill:
                        fill.pop(0)()
                    gates_unit(j - 1)
                    if fill:
                        fill.pop(0)()
                while fill:
                    fill.pop(0)()
                if t > 0:
                    finish_tile(t - 1)
                js0 = conv31_unit(0)
                gates_unit(3)
                if nxt:
                    P1_sq(t + 1)
                js1 = conv31_unit(1)
                conv31_stats(js0)
                js2 = conv31_unit(2)
                conv31_stats(js1)
                if nxt:
                    P1(t + 1)
                js3 = conv31_unit(3)
                conv31_stats(js2)
                lru_tail_act()
                if nxt:
                    W_xl(0)
                    W_xl(1)
                ln_stat_groups()
                if nxt:
                    W_xl(2)
                    W_xl(3)
                s_ln_stats()
                for j in range(4):
                    s_z(j)
                for j in range(4):
                    if nxt:
                        W_cc(j)
                    s_silu(j)
                    lru_tail_dve(j)
                if nxt:
                    W_gl(0)
                    W_gl(1)
                s_lru_sq()
                s_cc_sq()
                if nxt:
                    W_gl(2)
                    W_gl(3)
                s_lru_stats()
                s_cc_stats()
                s_yn_lru()
                s_yn_cc()
            for oc in range(KC):
                wout_unit(n_tiles - 1, oc)
            finish_tile(n_tiles - 1)

            Bwub.w = (sem_cv, S.latest[sem_cv], "dma:pool")
            Bwdb.w = Bwub.w
            S.barrier()
        with ExitStack() as pf:
            w_up_sb = sb("w_up_sb", [128, KC, DFF], BF16, pf)
            w_dn_sb = sb("w_dn_sb", [128, FC, D], BF16, pf)
            u2 = sb("u2", [128, KC, N], BF16, pf)
            sqf = sb("sqf", [128, 4, N], BF16, pf)
            scrf = sb("scrf", [128, 2, N], F32, pf)
            act = sb("act", [128, 16, N], BF16, pf)
            rl = sb("rl", [128, 4, N], BF16, pf)
            print("SBUF remaining after phase F alloc:", nc.sbuf_bytes_remaining)
            Bu2 = [_Buf("u2%d" % i) for i in range(KC)]
            Bsqf = [_Buf("sqf%d" % i) for i in range(4)]
            Bscrf = [_Buf("scrf%d" % i) for i in range(2)]
            Bact = [_Buf("act%d" % i) for i in range(16)]
            Brl = [_Buf("rl%d" % i) for i in range(4)]
            Bwu = [_Buf("wup%d" % i) for i in range(4)]
            Bwd = [_Buf("wdn%d" % i) for i in range(4)]
            sem_wu = [S.dma_sem("wup%d" % i) for i in range(4)]
            sem_wd = [S.dma_sem("wdn%d" % i) for i in range(4)]
            wub_v = wub.rearrange("(kc p) o -> p kc o", p=128)
            wdb_v = wdb.rearrange("(fc p) o -> p fc o", p=128)

            def ld_wu(q):
                S.dma("sp", sem_wu[q], w_up_sb[:, :, q * 1024:(q + 1) * 1024], wub_v[:, :, q * 1024:(q + 1) * 1024],
                      reads=[Bwub], writes=[Bwu[q]])

            def ld_wd(q):
                S.dma("sp", sem_wd[q], w_dn_sb[:, q * 8:(q + 1) * 8, :], wdb_v[:, q * 8:(q + 1) * 8, :],
                      reads=[Bwdb], writes=[Bwd[q]])

            ld_wu(0)
            ld_wu(1)
            ld_wd(0)
            ld_wd(1)
            ld_wu(2)
            ld_wu(3)
            ld_wd(2)
            ld_wd(3)

            psf_rr = [0]

            def next_psf():
                b = psf_rr[0] % 8
                psf_rr[0] += 1
                return b

            sqf_rr = [0]
            rl_rr = [0]

            def normF_A(h, Bhb):
                pb = next_psf()
                g = S.pe_begin([Bps[pb]])
                for kc in range(KC):
                    s = sqf_rr[0] % 4
                    sqf_rr[0] += 1
                    S.op("act", lambda e, kc=kc, s=s: e.activation(out=sqf[:, s, :], in_=h[:, kc, :], func=AF.Square),
                         reads=[Bhb], writes=[Bsqf[s]])
                    S.pe_item(g, mm(psum[pb][:, 0:N], ones1024[:], sqf[:, s, :], kc == 0, kc == KC - 1),
                              [Bsqf[s], Bconst], last=(kc == KC - 1))
                return pb

            def normF_B(pb, h, Bhb, gcol0, dst_fn, dst_bufs):
                S.op("act", lambda e: e.activation(out=scrf[:, 0, :], in_=psum[pb][:, 0:N], func=AF.Ln,
                                                   bias=dcol(CEPS), scale=1.0),
                     reads=[Bps[pb], Bconst], writes=[Bscrf[0]])
                S.op("act", lambda e: e.activation(out=scrf[:, 1, :], in_=scrf[:, 0, :], func=AF.Exp, scale=-0.5),
                     reads=[Bscrf[0]], writes=[Bscrf[1]])
                for kc in range(KC):
                    S.op("dve", lambda e, kc=kc: e.scalar_tensor_tensor(
                        out=dst_fn(kc), in0=h[:, kc, :], scalar=col(gcol0 + kc), in1=scrf[:, 1, :],
                        op0=ALU.mult, op1=ALU.mult), reads=[Bhb, Bscrf[1], Bconst], writes=[dst_bufs[kc]])

            perm = [n_tiles - 1] + list(range(n_tiles - 1))
            fbase = (n_tiles - 1) % 3

            def hb(pos):
                return (fbase + pos) % 3

            def S4(pos):
                b = hb(pos)
                pb = normF_A(hbuf[b], Bh[b])
                normF_B(pb, hbuf[b], Bh[b], G_MLP, lambda kc: u2[:, kc, :], Bu2)

            def S7_store(pos):
                b = hb(pos)
                t = perm[pos]
                h = hbuf[b]
                pb = normF_A(h, Bh[b])
                normF_B(pb, h, Bh[b], G_FIN, lambda kc: h[:, kc, :], [Bh[b]] * KC)
                if t == 0:
                    S.dma("sp", sem_hst[b], out_v[:, :, 0:N - NMETA], h[:, :, NMETA:N], reads=[Bh[b]])
                else:
                    S.dma("sp", sem_hst[b], out_v[:, :, t * N - NMETA:(t + 1) * N - NMETA], h[:], reads=[Bh[b]])

            def ld_h(pos):
                b = hb(pos)
                t = perm[pos]
                S.dma("sp", sem_hld[b], hbuf[b][:], h1_v[:, :, t * N:(t + 1) * N], writes=[Bh[b]])

            def up_group(f, fc):
                pb = next_psf()
                items = [(mm(psum[pb][:, 0:N], w_up_sb[:, kc, fc * 128:(fc + 1) * 128], u2[:, kc, :],
                             kc == 0, kc == KC - 1), [Bu2[kc], Bwu[fc // 8]]) for kc in range(KC)]
                S.pe_group(items, [Bps[pb]])
                r = rl_rr[0] % 4
                rl_rr[0] += 1
                S.op("act", lambda e: e.activation(out=rl[:, r, :], in_=psum[pb][:, 0:N], func=AF.Relu),
                     reads=[Bps[pb]], writes=[Brl[r]])
                S.op("pool" if f % 2 else "dve", lambda e: e.tensor_tensor(
                    out=act[:, f, :], in0=rl[:, r, :], in1=rl[:, r, :], op=ALU.mult),
                    reads=[Brl[r]], writes=[Bact[f]])

            def down_group(pos, half, oc):
                b = hb(pos)
                h = hbuf[b]
                pb = next_psf()
                items = [(mm(psum[pb][:, 0:N], w_dn_sb[:, half * 16 + f, oc * 128:(oc + 1) * 128],
                             act[:, f, :], f == 0, f == 15), [Bact[f], Bwd[(half * 16 + f) // 8]]) for f in range(16)]
                S.pe_group(items, [Bps[pb]])
                S.op("dve", lambda e: e.tensor_tensor(out=h[:, oc, :], in0=h[:, oc, :],
                                                      in1=psum[pb][:, 0:N], op=ALU.add),
                     reads=[Bh[b], Bps[pb]], writes=[Bh[b]])

            if n_tiles > 1:
                ld_h(1)
            S4(0)
            for pos in range(n_tiles):
                for half in range(2):
                    for f in range(16):
                        up_group(f, half * 16 + f)
                        if half == 0 and f == 3 and pos > 0:
                            S7_store(pos - 1)
                            if pos + 1 < n_tiles:
                                ld_h(pos + 1)
                    if half == 1 and pos + 1 < n_tiles:
                        S4(pos + 1)
                    for oc in range(KC):
                        down_group(pos, half, oc)
            S7_store(n_tiles - 1)
            S.barrier()
        print("sched waits:", S.nwaits, {k: v for k, v in S.latest.items()})
    return nc


def _host_prep(inp):
    f = np.float32
    x = np.asarray(inp["x"], f)
    B = x.shape[0]
    meta = np.asarray(inp["meta_tokens"], f)
    hTs = []
    for bi in range(B):
        hcat = np.concatenate([meta, x[bi]], axis=0)
        hTs.append(np.ascontiguousarray(hcat.T))
    pp = np.zeros((128, NPP), f)

    def chunked(v, n):
        return np.asarray(v, f).reshape(n, 128).T

    pp[:, G_MIX:G_MIX + 8] = chunked(inp["norm_mix_g"][0], 8)
    pp[:, G_MLP:G_MLP + 8] = chunked(inp["norm_mlp_g"][0], 8)
    pp[:, G_FIN:G_FIN + 8] = chunked(inp["norm_final_g"], 8)
    pp[:, G_OUT:G_OUT + 4] = chunked(inp["out_norm_lru_g"][0], 4)
    pp[:, G_OUT + 4:G_OUT + 8] = chunked(inp["out_norm_cc_g"][0], 4)
    c4 = np.asarray(inp["lru_conv_w"][0], f)
    pp[:, C4W:C4W + 16] = c4.reshape(4, 4, 128).transpose(2, 1, 0).reshape(128, 16)
    pp[:, C4B:C4B + 4] = chunked(inp["lru_conv_b"][0], 4)
    pp[:, GAB:GAB + 4] = chunked(inp["lru_gate_a_b"][0], 4)
    pp[:, GXB:GXB + 4] = chunked(inp["lru_gate_x_b"][0], 4)
    pp[:, APAR:APAR + 4] = chunked(inp["lru_a_param"][0], 4)
    c31 = np.asarray(inp["cc_conv_w"][0], f)
    pp[:, C31W:C31W + 124] = c31.reshape(31, 4, 128).transpose(2, 1, 0).reshape(128, 124)
    pp[:, C31B:C31B + 4] = chunked(inp["cc_conv_b"][0], 4)
    pp[:, LNG:LNG + 4] = chunked(inp["cc_ln_g"][0], 4)
    pp[:, LNB:LNB + 4] = chunked(inp["cc_ln_b"][0], 4)
    pp[:, IDENT:IDENT + 64] = np.concatenate([np.eye(64, dtype=f), np.eye(64, dtype=f)], axis=0)
    gwa = np.asarray(inp["lru_gate_a_w"][0], f)
    gwx = np.asarray(inp["lru_gate_x_w"][0], f)
    gw = np.zeros((128, 8, 128), f)
    for j in range(4):
        for hh in range(2):
            gw[hh * 64:(hh + 1) * 64, j, hh * 64:(hh + 1) * 64] = gwa[2 * j + hh]
            gw[hh * 64:(hh + 1) * 64, 4 + j, hh * 64:(hh + 1) * 64] = gwx[2 * j + hh]
    shared = {
        "w_in": np.ascontiguousarray(inp["w_in"][0], f),
        "w_out": np.ascontiguousarray(inp["w_out"][0], f),
        "w_up": np.ascontiguousarray(inp["w_mlp_up"][0], f),
        "w_down": np.ascontiguousarray(inp["w_mlp_down"][0], f),
        "gw": gw, "pp": pp,
    }
    return hTs, shared


def kernel(**inputs):
    hTs, shared = _host_prep(inputs)
    B = len(hTs)
    T = hTs[0].shape[1]
    assert T % N == 0
    n_tiles = T // N
    nc = bass.Bass("TRN2", target_bir_lowering=False)
    _build(nc, n_tiles)
    in_maps = [dict(shared, hT=hTs[bi]) for bi in range(B)]
    res = run_bass_kernel_spmd(nc, in_maps, core_ids=list(range(B)))
    if DEBUG:
        global LAST_DBG
        LAST_DBG = np.asarray(res.results[0]["dbg"])
    out = np.stack([np.ascontiguousarray(np.asarray(r["outT"]).T) for r in res.results], axis=0)
    return out.astype(np.float32)
```
